# Optimizing a Trainium2 kernel written in Bass

```python
import jax, jax.numpy as jnp
from jax import lax
import numpy as np

D_MODEL = 1024
BATCH = 8
SEQ = 4096
DEPTH = 4

N_HEADS = 16
HEAD_DIM = D_MODEL // N_HEADS
GRID_W = 64
CTX_LEN = 256
WIN_H = 8
WIN_W = 16
QB_W = 16
KB_W = QB_W + WIN_W
CONV_K = 31
D_FF = -(-8 * D_MODEL // (3 * 256)) * 256
N_MIXERS = 2
ALPHA = (2 * DEPTH) ** 0.25
BETA = (8 * DEPTH) ** -0.25
LN_EPS = 1e-5
NEG_INF = -1e30

kernel_name = "hybrid_natten_conformer_diffusion_trunk"


def layer_norm(x, g, b):
    xf = x.astype(jnp.float32)
    mu = jnp.mean(xf, axis=-1, keepdims=True)
    var = jnp.mean(jnp.square(xf - mu), axis=-1, keepdims=True)
    y = (xf - mu) * lax.rsqrt(var + LN_EPS)
    return (y * g.astype(jnp.float32) + b.astype(jnp.float32)).astype(x.dtype)


def _column_structure():
    n_cb = GRID_W // QB_W
    qcol = np.arange(n_cb)[:, None] * QB_W + np.arange(QB_W)[None, :]
    qstart = np.clip(qcol - WIN_W // 2, 0, GRID_W - WIN_W)
    band0 = np.minimum(qstart[:, 0], GRID_W - KB_W)
    kcol = band0[:, None] + np.arange(KB_W)[None, :]
    mask = (kcol[:, None, :] >= qstart[:, :, None]) & (kcol[:, None, :] < qstart[:, :, None] + WIN_W)
    off = np.clip(kcol[:, None, :] - qcol[:, :, None] + WIN_W - 1, 0, 2 * WIN_W - 2)
    return kcol, mask, off


def neighborhood_attention(q, k, v, k_ctx, v_ctx, rpb):
    b, n, h, dh = q.shape
    rows = n // GRID_W
    kh = min(WIN_H, rows)
    n_cb = GRID_W // QB_W
    kcol, cmask, coff = _column_structure()
    kcol = jnp.asarray(kcol.reshape(-1), dtype=jnp.int32)
    cmask = jnp.asarray(cmask)
    coff = jnp.asarray(coff, dtype=jnp.int32)
    k_grid = k.reshape(b, rows, GRID_W, h, dh)
    v_grid = v.reshape(b, rows, GRID_W, h, dh)
    q_rows = jnp.moveaxis(q.reshape(b, rows, n_cb, QB_W, h, dh), 1, 0)
    scale = HEAD_DIM ** -0.5
    n_loc = kh * KB_W

    def one_row(args):
        r, q_r = args
        r0 = jnp.clip(r - kh // 2, 0, rows - kh)
        k_r = lax.dynamic_slice_in_dim(k_grid, r0, kh, axis=1)
        v_r = lax.dynamic_slice_in_dim(v_grid, r0, kh, axis=1)
        k_band = jnp.take(k_r, kcol, axis=2).reshape(b, kh, n_cb, KB_W, h, dh)
        v_band = jnp.take(v_r, kcol, axis=2).reshape(b, kh, n_cb, KB_W, h, dh)
        row_off = r0 + jnp.arange(kh, dtype=jnp.int32) - r + (WIN_H - 1)
        bias = rpb[:, row_off[None, None, :, None], coff[:, :, None, :]]
        s_loc = jnp.einsum('bnqhd,binmhd->bhnqim', q_r, k_band,
                           preferred_element_type=jnp.float32) * scale + bias.astype(jnp.float32)
        s_loc = jnp.where(cmask[:, :, None, :], s_loc, NEG_INF)
        s_ctx = jnp.einsum('bnqhd,bchd->bhnqc', q_r, k_ctx,
                           preferred_element_type=jnp.float32) * scale
        s = jnp.concatenate([s_loc.reshape(b, h, n_cb, QB_W, n_loc), s_ctx], axis=-1)
        p = jax.nn.softmax(s, axis=-1).astype(v.dtype)
        p_loc = p[..., :n_loc].reshape(b, h, n_cb, QB_W, kh, KB_W)
        p_ctx = p[..., n_loc:]
        return (jnp.einsum('bhnqim,binmhd->bnqhd', p_loc, v_band)
                + jnp.einsum('bhnqc,bchd->bnqhd', p_ctx, v_ctx))

    out = lax.map(one_row, (jnp.arange(rows, dtype=jnp.int32), q_rows))
    return jnp.moveaxis(out, 0, 1).reshape(b, n, h * dh)


def attn_mixer(h_lat, h_ctx, w_qkv, b_qkv, w_o, b_o, rpb, ctx_out):
    b, n, d = h_lat.shape
    cl = h_ctx.shape[1]
    qkv = h_lat @ w_qkv + b_qkv
    q_l, k_l, v_l = [t.reshape(b, n, N_HEADS, HEAD_DIM) for t in jnp.split(qkv, 3, axis=-1)]
    if ctx_out:
        qkv_c = h_ctx @ w_qkv + b_qkv
        q_c, k_c, v_c = [t.reshape(b, cl, N_HEADS, HEAD_DIM) for t in jnp.split(qkv_c, 3, axis=-1)]
    else:
        kv_c = h_ctx @ w_qkv[:, d:] + b_qkv[d:]
        k_c, v_c = [t.reshape(b, cl, N_HEADS, HEAD_DIM) for t in jnp.split(kv_c, 2, axis=-1)]
    y_l = neighborhood_attention(q_l, k_l, v_l, k_c, v_c, rpb) @ w_o + b_o
    if not ctx_out:
        return y_l, None
    s = jnp.einsum('bqhd,bkhd->bhqk', q_c, k_c, preferred_element_type=jnp.float32) * HEAD_DIM ** -0.5
    p = jax.nn.softmax(s, axis=-1).astype(v_c.dtype)
    o_c = jnp.einsum('bhqk,bkhd->bqhd', p, v_c).reshape(b, cl, d)
    return y_l, o_c @ w_o + b_o


def conformer_conv(h, w_pw1, b_pw1, w_dw, b_dw, ln_g, ln_b, w_pw2, b_pw2):
    d = h.shape[-1]
    a, g = jnp.split(h @ w_pw1 + b_pw1, 2, axis=-1)
    u = a * jax.nn.sigmoid(g)
    pad = CONV_K // 2
    u = lax.conv_general_dilated(u, w_dw[:, None, :], window_strides=(1,), padding=((pad, pad),),
                                 dimension_numbers=('NWC', 'WIO', 'NWC'), feature_group_count=d) + b_dw
    u = jax.nn.silu(layer_norm(u, ln_g, ln_b))
    return u @ w_pw2 + b_pw2


def swiglu(h, w1, w3, w2):
    return (jax.nn.silu(h @ w1) * (h @ w3)) @ w2


def setup_inputs(seed: int = 0) -> dict:
    key = jax.random.key(seed)
    ks = jax.random.split(key, 26)
    n_a = len([i for i in range(DEPTH) if i % N_MIXERS == 0])
    n_b = DEPTH - n_a
    D = D_MODEL

    def nrm(k, shape, scale):
        return jax.random.normal(k, shape, jnp.float32) * scale

    def gain(k, shape):
        return 1.0 + 0.02 * jax.random.normal(k, shape, jnp.float32)

    return {
        "x": nrm(ks[0], (BATCH, SEQ, D), 1.0),
        "c": nrm(ks[1], (BATCH, D), 1.0),
        "ctx": nrm(ks[2], (BATCH, CTX_LEN, D), 1.0),
        "c_ctx": nrm(ks[3], (D,), 1.0),
        "w_ada": nrm(ks[4], (DEPTH, D, 6 * D), D ** -0.5),
        "b_ada": nrm(ks[5], (DEPTH, 6 * D), 0.01),
        "ln_mix_g": gain(ks[6], (DEPTH, D)),
        "ln_mix_b": nrm(ks[7], (DEPTH, D), 0.01),
        "ln_ffn_g": gain(ks[8], (DEPTH, D)),
        "ln_ffn_b": nrm(ks[9], (DEPTH, D), 0.01),
        "attn_w_qkv": nrm(ks[10], (n_a, D, 3 * D), D ** -0.5),
        "attn_b_qkv": nrm(ks[11], (n_a, 3 * D), 0.01),
        "attn_w_o": nrm(ks[12], (n_a, D, D), BETA * D ** -0.5),
        "attn_b_o": nrm(ks[13], (n_a, D), 0.01),
        "attn_rpb": nrm(ks[14], (n_a, N_HEADS, 2 * WIN_H - 1, 2 * WIN_W - 1), 0.1),
        "conv_w_pw1": nrm(ks[15], (n_b, D, 2 * D), D ** -0.5),
        "conv_b_pw1": nrm(ks[16], (n_b, 2 * D), 0.01),
        "conv_w_dw": nrm(ks[17], (n_b, CONV_K, D), CONV_K ** -0.5),
        "conv_b_dw": nrm(ks[18], (n_b, D), 0.01),
        "conv_ln_g": gain(ks[19], (n_b, D)),
        "conv_ln_b": nrm(ks[20], (n_b, D), 0.01),
        "conv_w_pw2": nrm(ks[21], (n_b, D, D), BETA * D ** -0.5),
        "conv_b_pw2": nrm(ks[22], (n_b, D), 0.01),
        "ffn_w1": nrm(ks[23], (DEPTH, D, D_FF), D ** -0.5),
        "ffn_w3": nrm(ks[24], (DEPTH, D, D_FF), D ** -0.5),
        "ffn_w2": nrm(ks[25], (DEPTH, D_FF, D), BETA * D_FF ** -0.5),
    }


def reference(x, c, ctx, c_ctx, w_ada, b_ada, ln_mix_g, ln_mix_b, ln_ffn_g, ln_ffn_b,
              attn_w_qkv, attn_b_qkv, attn_w_o, attn_b_o, attn_rpb,
              conv_w_pw1, conv_b_pw1, conv_w_dw, conv_b_dw, conv_ln_g, conv_ln_b, conv_w_pw2, conv_b_pw2,
              ffn_w1, ffn_w3, ffn_w2):
    last_attn = max(i for i in range(DEPTH) if i % N_MIXERS == 0)
    s_lat = jax.nn.silu(c)
    s_ctx = jax.nn.silu(c_ctx)[None, :]
    xc = ctx
    for i in range(DEPTH):
        slot = i // N_MIXERS
        ctx_in = i <= last_attn
        ctx_live = i < last_attn
        sh1, sc1, g1, sh2, sc2, g2 = jnp.split((s_lat @ w_ada[i] + b_ada[i])[:, None, :], 6, axis=-1)
        h = x * (1 + sc1) + sh1
        hc = None
        if ctx_in:
            csh1, csc1, cg1, csh2, csc2, cg2 = jnp.split((s_ctx @ w_ada[i] + b_ada[i])[:, None, :], 6, axis=-1)
            hc = xc * (1 + csc1) + csh1
        if i % N_MIXERS == 0:
            y, yc = attn_mixer(h, hc, attn_w_qkv[slot], attn_b_qkv[slot], attn_w_o[slot], attn_b_o[slot],
                               attn_rpb[slot], ctx_live)
        else:
            conv_args = (conv_w_pw1[slot], conv_b_pw1[slot], conv_w_dw[slot], conv_b_dw[slot],
                         conv_ln_g[slot], conv_ln_b[slot], conv_w_pw2[slot], conv_b_pw2[slot])
            y = conformer_conv(h, *conv_args)
            yc = conformer_conv(hc, *conv_args) if ctx_live else None
        x = layer_norm(ALPHA * x + g1 * y, ln_mix_g[i], ln_mix_b[i])
        h = x * (1 + sc2) + sh2
        x = layer_norm(ALPHA * x + g2 * swiglu(h, ffn_w1[i], ffn_w3[i], ffn_w2[i]), ln_ffn_g[i], ln_ffn_b[i])
        if ctx_live:
            xc = layer_norm(ALPHA * xc + cg1 * yc, ln_mix_g[i], ln_mix_b[i])
            hc = xc * (1 + csc2) + csh2
            xc = layer_norm(ALPHA * xc + cg2 * swiglu(hc, ffn_w1[i], ffn_w3[i], ffn_w2[i]),
                            ln_ffn_g[i], ln_ffn_b[i])
    return x
```

```python
import numpy as np
from contextlib import ExitStack
import concourse.bass as bass
import concourse.mybir as mybir
from concourse.bass_utils import run_bass_kernel_spmd

F32 = mybir.dt.float32
BF16 = mybir.dt.bfloat16
ALU = mybir.AluOpType
AF = mybir.ActivationFunctionType
AX = mybir.AxisListType

ENGS = ("pe", "act", "dve", "pool", "sp")


class Buf:
    __slots__ = ("name", "lw", "rd", "dsem", "dcnt")

    def __init__(self, name):
        self.name = name
        self.lw = None
        self.rd = []
        self.dsem = None
        self.dcnt = 0


class MK:
    def __init__(self, nc, stack):
        self.nc = nc
        self.stack = stack
        self.sems = {}
        for e in ENGS:
            self.sems["E_" + e] = stack.enter_context(nc.semaphore("s_" + e))
        self.cnt = {e: 0 for e in ENGS}
        self.prog = {e: [] for e in ENGS}
        self.known = {e: {} for e in ENGS}
        self.dtot = {}
        self.nbuf = 0

    def buf(self, name=None):
        self.nbuf += 1
        return Buf("%s_%d" % (name or "b", self.nbuf))

    def _dsem(self, b, eng):
        if b.dsem is None:
            b.dsem = {}
            b.dcnt = {}
        if eng not in b.dsem:
            key = "D_" + b.name + "_" + eng
            self.sems[key] = self.stack.enter_context(self.nc.semaphore("d%d" % len(self.sems)))
            b.dsem[eng] = key
            b.dcnt[eng] = 0
        return b.dsem[eng]

    def _need(self, eng, reads, writes):
        need = {}

        def add(ev, raw):
            if ev is None:
                return
            k, v, e = ev
            if e == eng and k == "E_" + eng:
                if eng in ("pe", "sp"):
                    return
                if self.cnt[eng] - v >= (3 if raw else 6):
                    return
            if need.get(k, 0) < v:
                need[k] = v

        for b in reads:
            add(b.lw, True)
        for b in writes:
            add(b.lw, True)
            for ev in b.rd:
                add(ev, False)
        waits = []
        kn = self.known[eng]
        for k, v in need.items():
            if kn.get(k, 0) < v:
                kn[k] = v
                waits.append((k, v))
        return waits

    def _record(self, ev, reads, writes):
        for b in reads:
            b.rd.append(ev)
            if len(b.rd) > 24:
                best = {}
                for e2 in b.rd:
                    if best.get(e2[0], (0, 0, 0))[1] < e2[1]:
                        best[e2[0]] = e2
                b.rd = list(best.values())
        for b in writes:
            b.lw = ev
            b.rd = []

    def op(self, eng, fn, reads=(), writes=()):
        waits = self._need(eng, reads, writes)
        self.cnt[eng] += 1
        ev = ("E_" + eng, self.cnt[eng], eng)
        self.prog[eng].append((waits, fn, ("E_" + eng, 1)))
        self._record(ev, reads, writes)
        return ev

    def dma(self, eng, out_ap, in_ap, sbuf_buf, reads=(), writes=()):
        waits = self._need(eng, reads, writes)
        key = self._dsem(sbuf_buf, eng)
        sbuf_buf.dcnt[eng] += 16
        self.dtot[key] = sbuf_buf.dcnt[eng]
        ev = (key, sbuf_buf.dcnt[eng], "dma")

        def fn(h, out_ap=out_ap, in_ap=in_ap):
            return h.dma_start(out=out_ap, in_=in_ap)

        self.prog[eng].append((waits, fn, (key, 16)))
        self._record(ev, reads, writes)
        return ev

    def barrier(self):
        for e in ENGS:
            waits = []
            kn = self.known[e]
            for x in ENGS:
                if x == e or self.cnt[x] == 0:
                    continue
                k = "E_" + x
                if kn.get(k, 0) < self.cnt[x]:
                    kn[k] = self.cnt[x]
                    waits.append((k, self.cnt[x]))
            for k, v in self.dtot.items():
                if kn.get(k, 0) < v:
                    kn[k] = v
                    waits.append((k, v))
            if waits:
                self.prog[e].append((waits, None, None))

    def final_wait(self, eng, bufs):
        waits = self._need(eng, bufs, bufs)
        if waits:
            self.prog[eng].append((waits, None, None))

    def emit(self):
        nc = self.nc
        sems = self.sems
        prog = self.prog

        waited = {"E_" + e: set() for e in ENGS}
        for name in ENGS:
            for waits, fn, inc in prog[name]:
                for k, v in waits:
                    if k in waited:
                        waited[k].add(v)
        rank = {k: {v: i + 1 for i, v in enumerate(sorted(vs))} for k, vs in waited.items()}
        self.max_sem = {k: len(vs) for k, vs in waited.items()}

        def run(e, name):
            idx = 0
            for waits, fn, inc in prog[name]:
                for k, v in waits:
                    e.wait_ge(sems[k], rank[k][v] if k in rank else v)
                if fn is not None:
                    ins = fn(e)
                    if inc[0] in rank:
                        idx += 1
                        if idx in rank[inc[0]]:
                            ins.then_inc(sems[inc[0]], 1)
                    else:
                        ins.then_inc(sems[inc[0]], inc[1])

        with nc.Block() as block:
            @block.tensor
            def _(e):
                run(e, "pe")

            @block.scalar
            def _(e):
                run(e, "act")

            @block.vector
            def _(e):
                run(e, "dve")

            @block.gpsimd
            def _(e):
                run(e, "pool")

            @block.sync
            def _(e):
                run(e, "sp")


class T:
    def __init__(self, t, b):
        self.t = t
        self.b = b

    def __getitem__(self, k):
        return self.t[k]


class Cfg:
    def __init__(self, D=1024, SEQ=4096, CTX=256, DFF=2816, DEPTH=4):
        self.D, self.SEQ, self.CTX, self.DFF, self.DEPTH = D, SEQ, CTX, DFF, DEPTH
        self.NCH = D // 128
        self.NH = D // 64
        self.NFC = DFF // 128
        self.ROWS = SEQ // 64
        self.NT = SEQ // 128
        self.NTC = CTX // 128
        self.BLK = 256
        self.NTOK = SEQ + CTX
        self.ALPHA = (2 * DEPTH) ** 0.25
        self.n_attn = len([i for i in range(DEPTH) if i % 2 == 0])
        self.n_conv = DEPTH - self.n_attn
        self.last_attn = max(i for i in range(DEPTH) if i % 2 == 0)
        self.KH = min(8, self.ROWS)


NEG = -1.0e4


def attn_patterns(cfg):
    ROWS, KH = cfg.ROWS, cfg.KH
    sigs = {}
    per_tile = []
    for t in range(cfg.NT):
        rows = (2 * t, 2 * t + 1)
        r0s = [int(np.clip(r - KH // 2, 0, ROWS - KH)) for r in rows]
        j_lo = min(r0s) // 2
        j_hi = (max(r0s) + KH - 1) // 2
        lst = []
        for j in range(j_lo, j_hi + 1):
            sig = []
            for qi, r in enumerate(rows):
                for kr in (2 * j, 2 * j + 1):
                    ok = r0s[qi] <= kr < r0s[qi] + KH
                    sig.append((kr - r + 7) if ok else None)
            sig = tuple(sig)
            if all(s is None for s in sig):
                continue
            if sig not in sigs:
                sigs[sig] = len(sigs)
            lst.append((j, sigs[sig]))
        per_tile.append(lst)
    pats = [None] * len(sigs)
    for s, i in sigs.items():
        pats[i] = s
    return pats, per_tile


def make_btab(rpb, cfg):
    pats, _ = attn_patterns(cfg)
    H = cfg.NH
    c = np.arange(64)
    qs = np.clip(c - 8, 0, 64 - 16)
    kc = np.arange(64)
    colvalid = (kc[:, None] >= qs[None, :]) & (kc[:, None] < qs[None, :] + 16)
    coff = np.clip(kc[:, None] - c[None, :] + 15, 0, 30)
    out = np.full((len(pats), 128, H, 128), NEG, dtype=np.float32)
    for p, sig in enumerate(pats):
        for qi in range(2):
            for ki in range(2):
                ro = sig[qi * 2 + ki]
                if ro is None:
                    continue
                g = rpb[:, ro, :][:, coff]
                g = np.where(colvalid[None], g, np.float32(NEG)).astype(np.float32)
                out[p, ki * 64:(ki + 1) * 64, :, qi * 64:(qi + 1) * 64] = np.transpose(g, (1, 0, 2))
    return np.ascontiguousarray(out.reshape(len(pats), 128, H * 128))


def build_nc(cfg):
    D, NCH, NH, NFC, BLK = cfg.D, cfg.NCH, cfg.NH, cfg.NFC, cfg.BLK
    SEQ, CTX, NTOK, DEPTH, DFF = cfg.SEQ, cfg.CTX, cfg.NTOK, cfg.DEPTH, cfg.DFF
    ALPHA = float(cfg.ALPHA)
    pats, per_tile = attn_patterns(cfg)
    NPAT = len(pats)
    nc = bass.Bass("TRN2", target_bir_lowering=False)

    def din(name, shape):
        return nc.dram_tensor(name, list(shape), F32, kind="ExternalInput").ap()

    xT_d = din("xT", [D, NTOK])
    cc_d = din("cc", [2, D])
    w_ada_d = din("w_ada", [DEPTH, D, 6 * D])
    b_ada_d = din("b_ada", [DEPTH, 6 * D])
    ln_d = {k: din(k, [DEPTH, D]) for k in ("ln_mix_g", "ln_mix_b", "ln_ffn_g", "ln_ffn_b")}
    wqkv_d = din("attn_w_qkv", [cfg.n_attn, D, 3 * D])
    bqkv_d = din("attn_b_qkv", [cfg.n_attn, 3 * D])
    wo_d = din("attn_w_o", [cfg.n_attn, D, D])
    bo_d = din("attn_b_o", [cfg.n_attn, D])
    btab_d = din("btab", [cfg.n_attn, NPAT, 128, NH * 128])
    wpw1_d = din("conv_w_pw1", [cfg.n_conv, D, 2 * D])
    bpw1_d = din("conv_b_pw1", [cfg.n_conv, 2 * D])
    wdw_d = din("conv_w_dw", [cfg.n_conv, 31, D])
    bdw_d = din("conv_b_dw", [cfg.n_conv, D])
    clng_d = din("conv_ln_g", [cfg.n_conv, D])
    clnb_d = din("conv_ln_b", [cfg.n_conv, D])
    wpw2_d = din("conv_w_pw2", [cfg.n_conv, D, D])
    bpw2_d = din("conv_b_pw2", [cfg.n_conv, D])
    w1_d = din("ffn_w1", [DEPTH, D, DFF])
    w3_d = din("ffn_w3", [DEPTH, D, DFF])
    w2_d = din("ffn_w2", [DEPTH, DFF, D])
    ident_d = din("ident", [128, 128])
    out_d = nc.dram_tensor("outT", [D, SEQ], F32, kind="ExternalOutput").ap()
    XIN_v = xT_d.rearrange("(c p) t -> p c t", p=128)
    OUT_v = out_d.rearrange("(c p) t -> p c t", p=128)
    XT_d = nc.dram_tensor("XT", [D, NTOK], F32).ap()
    UT_d = nc.dram_tensor("UT", [D, SEQ + 30], BF16).ap()
    UC_d = nc.dram_tensor("UC", [D, CTX + 30], BF16).ap()
    XT_v = XT_d.rearrange("(c p) t -> p c t", p=128)
    UT_v = UT_d.rearrange("(c p) t -> p c t", p=128)
    UC_v = UC_d.rearrange("(c p) t -> p c t", p=128)

    with ExitStack() as st:
        mk = MK(nc, st)
        cur = [st]

        sbn = [0]

        def sb(name, shape, dt=F32):
            sbn[0] += 1
            return T(cur[0].enter_context(nc.sbuf_tensor("s%d_%s" % (sbn[0], name), list(shape), dt)), mk.buf(name))

        def B(xs):
            return [r.b if isinstance(r, T) else r for r in xs]

        pending = []
        bg_pending = []
        defer = [False]
        defer_bg = [False]

        def drain_bg(k=None):
            n = len(bg_pending) if k is None else min(k, len(bg_pending))
            open_grp = False
            i = 0
            while bg_pending and (i < n or open_grp):
                it = bg_pending.pop(0)
                i += 1
                if it[0] == "op":
                    mk.op(*it[1:5])
                    if it[5] is not None:
                        open_grp = not it[5][2]
                else:
                    mk.dma(*it[1:])

        def op(eng, fn, rd, wr, tag=None):
            if defer_bg[0]:
                bg_pending.append(("op", eng, fn, B(rd), B(wr), tag))
                return None
            if defer[0]:
                pending.append(("op", eng, fn, B(rd), B(wr)))
                return None
            return mk.op(eng, fn, B(rd), B(wr))

        def dma(eng, out_ap, in_ap, sbt, rd, wr):
            sb_ = sbt.b if isinstance(sbt, T) else sbt
            if defer_bg[0]:
                bg_pending.append(("dma", eng, out_ap, in_ap, sb_, B(rd), B(wr)))
                return None
            if defer[0]:
                pending.append(("dma", eng, out_ap, in_ap, sb_, B(rd), B(wr)))
                return None
            return mk.dma(eng, out_ap, in_ap, sb_, B(rd), B(wr))

        def drain(k=None):
            n = len(pending) if k is None else min(k, len(pending))
            for _ in range(n):
                it = pending.pop(0)
                if it[0] == "op":
                    mk.op(*it[1:])
                else:
                    mk.dma(*it[1:])

        def ACT(out, in_, func, rd, wr, **kw):
            op("act", lambda e: e.activation(out=out, in_=in_, func=func, **kw), rd, wr)

        def MM(out, lhsT, rhs, start, stop, rd, wr):
            op("pe", lambda e: e.matmul(out, lhsT=lhsT, rhs=rhs, start=start, stop=stop), rd, wr, tag=("mm", start, stop))

        def STT(out, in0, scalar, in1, op0, op1, rd, wr):
            op("dve", lambda e: e.scalar_tensor_tensor(out=out, in0=in0, scalar=scalar, in1=in1, op0=op0, op1=op1), rd, wr)

        def TT(out, in0, in1, o, rd, wr):
            op("dve", lambda e: e.tensor_tensor(out=out, in0=in0, in1=in1, op=o), rd, wr)

        def TSA(out, in0, s1, rd, wr):
            op("dve", lambda e: e.tensor_scalar_add(out=out, in0=in0, scalar1=s1), rd, wr)

        def TSM(out, in0, s1, rd, wr):
            op("dve", lambda e: e.tensor_scalar_mul(out=out, in0=in0, scalar1=s1), rd, wr)

        def CP(out, in_, rd, wr):
            op("dve", lambda e: e.tensor_copy(out=out, in_=in_), rd, wr)

        def RECIP(out, in_, rd, wr):
            op("dve", lambda e: e.reciprocal(out=out, in_=in_), rd, wr)

        def MEMSET(ap, val, wr):
            op("dve", lambda e: e.memset(ap, val), [], wr)

        def REDC(out, in3, rd, wr):
            v = in3.rearrange("p c t -> p t c")
            op("dve", lambda e: e.tensor_reduce(out=out, in_=v, axis=AX.X, op=ALU.add), rd, wr)

        D_in = mk.buf("Din")
        D_XT = [mk.buf("XT%d" % i) for i in range(NTOK // BLK)]
        D_UT = mk.buf("UT")
        D_UC = mk.buf("UC")
        D_pad = mk.buf("Upad")
        D_out = mk.buf("Dout")

        PS = [T(st.enter_context(nc.psum_tensor("ps%d" % i, [128, 512], F32)), mk.buf("ps%d" % i)) for i in range(8)]
        rot_state = {}

        def rot(name, banks):
            i = rot_state.get(name, 0)
            rot_state[name] = i + 1
            return PS[banks[i % len(banks)]]

        ident = sb("ident", [128, 128])
        identb = sb("identb", [128, 128], BF16)
        onesb = sb("onesb", [128, 128], BF16)
        epsc = sb("epsc", [128, 1])
        zcol = sb("zcol", [128, NCH])
        dma("sp", ident[:], ident_d, ident, [D_in], [ident])
        CP(identb[:], ident[:], [ident], [identb])
        MEMSET(onesb[:], 1.0 / D, [onesb])
        MEMSET(epsc[:], 1e-5, [epsc])
        MEMSET(zcol[:], 0.0, [zcol])

        PV = [sb("pv%d" % l, [128, 128]) for l in range(DEPTH)]
        WDW = [sb("wdw%d" % s, [128, 31 * NCH]) for s in range(cfg.n_conv)]
        MOD = [sb("mod%d" % l, [128, 6 * NCH, 2]) for l in range(DEPTH)]
        DER = [sb("der%d" % l, [128, 2, 4, NCH]) for l in range(DEPTH)]
        xb2 = [sb("xb%d" % i, [128, NCH, BLK]) for i in range(2)]
        hT = sb("hT", [128, NCH, BLK], BF16)
        sqb = sb("sqb", [128, NCH, BLK])
        s12 = sb("s12", [128, 2, BLK])
        s12h = sb("s12h", [128, 2, BLK], BF16)
        s12l = sb("s12l", [128, 2, BLK], BF16)
        s12r = sb("s12r", [128, 2, BLK])
        lnm = sb("lnm", [128, BLK])
        lnr = sb("lnr", [128, BLK])
        sg = [sb("sg%d" % i, [128, BLK]) for i in range(2)]
        Sbf = sb("Sbf", [128, NCH, 2], BF16)
        col = []

        def vec_rows(ap1d, width=128):
            return ap1d.rearrange("(n w) -> n w", w=width)

        with ExitStack() as pst:
            cur[0] = pst
            zpad = sb("zpad", [128, NCH, 16], BF16)
            MEMSET(zpad[:], 0.0, [zpad])
            for (v_, n_) in ((UT_v, SEQ), (UC_v, CTX)):
                dma("sp", v_[:, :, 0:15], zpad[:, :, 0:15], zpad, [zpad], [D_pad])
                dma("sp", v_[:, :, 15 + n_:30 + n_], zpad[:, :, 0:15], zpad, [zpad], [D_pad])
            stage = sb("stage", [128, 128])
            sp3 = [sb("sp3_%d" % i, [128, 128], BF16) for i in range(3)]
            spr = sb("spr", [128, 128])

            def rows_to_cols(row_specs, width, dest, dest_col0=0):
                r = 0
                for ap in row_specs:
                    n = ap.shape[0]
                    dma("sp", stage[r:r + n, 0:width], ap, stage, [D_in], [stage])
                    r += n
                R = r
                src = stage[0:R, 0:width]
                CP(sp3[0][0:R, 0:width], src, [stage], [sp3[0]])
                TT(spr[0:R, 0:width], src, sp3[0][0:R, 0:width], ALU.subtract, [stage, sp3[0]], [spr])
                CP(sp3[1][0:R, 0:width], spr[0:R, 0:width], [spr], [sp3[1]])
                TT(spr[0:R, 0:width], spr[0:R, 0:width], sp3[1][0:R, 0:width], ALU.subtract, [spr, sp3[1]], [spr])
                CP(sp3[2][0:R, 0:width], spr[0:R, 0:width], [spr], [sp3[2]])
                ps = rot("misc", [0, 1, 2])
                for i in range(3):
                    MM(ps[0:width, 0:R], sp3[i][0:R, 0:width], identb[0:R, 0:R], i == 0, i == 2, [sp3[i], identb], [ps])
                CP(dest[0:width, dest_col0:dest_col0 + R], ps[0:width, 0:R], [ps], [dest])

            for l in range(DEPTH):
                slot = l // 2
                specs = [("b_ada", vec_rows(b_ada_d[l]))]
                for k in ("ln_mix_g", "ln_mix_b", "ln_ffn_g", "ln_ffn_b"):
                    specs.append((k, vec_rows(ln_d[k][l])))
                if l % 2 == 0:
                    specs.append(("b_q", vec_rows(bqkv_d[slot][0:D])))
                    specs.append(("b_k", vec_rows(bqkv_d[slot][D:2 * D])))
                    specs.append(("b_v", vec_rows(bqkv_d[slot][2 * D:3 * D])))
                    specs.append(("b_out", vec_rows(bo_d[slot])))
                else:
                    specs.append(("b_pw1", vec_rows(bpw1_d[slot])))
                    specs.append(("b_dw", vec_rows(bdw_d[slot])))
                    specs.append(("cln_g", vec_rows(clng_d[slot])))
                    specs.append(("cln_b", vec_rows(clnb_d[slot])))
                    specs.append(("b_out", vec_rows(bpw2_d[slot])))
                cm = {}
                r = 0
                for nm, ap in specs:
                    cm[nm] = r
                    r += ap.shape[0]
                assert r <= 128, r
                col.append(cm)
                rows_to_cols([ap for _, ap in specs], 128, PV[l])
                if l % 2 == 1:
                    wrows = wdw_d[slot].rearrange("k (c w) -> (k c) w", w=128)
                    tot = 31 * NCH
                    r0 = 0
                    while r0 < tot:
                        n = min(124, tot - r0)
                        rows_to_cols([wrows[r0:r0 + n]], 128, WDW[slot], r0)
                        r0 += n

            Sst = sb("Sst", [128, 2 * NCH])
            rows_to_cols([vec_rows(cc_d[0]), vec_rows(cc_d[1])], 128, Sst)
            ACT(Sbf[:].rearrange("p c r -> p r c"), Sst[:, 0:2 * NCH].rearrange("p (r c) -> p r c", r=2), AF.Silu, [Sst], [Sbf])
            SLABW = 512
            nslab = 6 * D // SLABW

            def ada_layer(l, slab_list, ps, off):
                wv = w_ada_d[l].rearrange("(k p) n -> p k n", p=128)
                for s_i in range(nslab):
                    sl = slab_list[s_i % len(slab_list)]
                    dma("pool", sl[:], wv[:, :, s_i * SLABW:(s_i + 1) * SLABW], sl, [D_in], [sl])
                    for c4 in range(SLABW // 128):
                        cc = s_i * (SLABW // 128) + c4
                        for k in range(NCH):
                            MM(ps[:, off + cc * 2:off + cc * 2 + 2], sl[:, k, c4 * 128:(c4 + 1) * 128], Sbf[:, k, :], k == 0, k == NCH - 1, [sl, Sbf], [ps])
                for r in range(2):
                    TT(MOD[l][:, :, r], ps[:, off:off + 12 * NCH].rearrange("p (c r) -> p c r", r=2)[:, :, r],
                       PV[l][:, col[l]["b_ada"]:col[l]["b_ada"] + 6 * NCH], ALU.add, [ps, PV[l]], [MOD[l]])
                for r in range(2):
                    TSA(DER[l][:, r, 0, :], MOD[l][:, 1 * NCH:2 * NCH, r], 1.0, [MOD[l]], [DER[l]])
                    TSA(DER[l][:, r, 1, :], MOD[l][:, 4 * NCH:5 * NCH, r], 1.0, [MOD[l]], [DER[l]])
                    TT(DER[l][:, r, 2, :], MOD[l][:, 2 * NCH:3 * NCH, r], PV[l][:, col[l]["b_out"]:col[l]["b_out"] + NCH], ALU.mult, [MOD[l], PV[l]], [DER[l]])
                    if l % 2 == 0:
                        TSM(DER[l][:, r, 3, :], PV[l][:, col[l]["b_q"]:col[l]["b_q"] + NCH], 0.125, [PV[l]], [DER[l]])

            slabs = [sb("slab%d" % i, [128, NCH, SLABW], BF16) for i in range(2)]
            ada_layer(0, slabs, PS[3], 0)
            ada_fn = [ada_layer]
            drain()
            mk.barrier()
        cur[0] = st

        def mcol(l, which, c, r):
            return MOD[l][:, which * NCH + c:which * NCH + c + 1, r]

        def pvcol(l, name, c):
            o = col[l][name] + c
            return PV[l][:, o:o + 1]

        xb_i = [0]
        stat_banks = [[6, 7]]

        xbl = [xb2[0], xb2[1]]
        xsrc = [XIN_v]

        def load_xblock(bi, n):
            xb = xbl[xb_i[0] % len(xbl)]
            xb_i[0] += 1
            dma("sp", xb[:, :, 0:n], xsrc[0][:, :, bi * BLK:bi * BLK + n], xb, [D_XT[bi]], [xb])
            return xb

        def store_xblock(xb, bi, n):
            dma("sp", XT_v[:, :, bi * BLK:bi * BLK + n], xb[:, :, 0:n], xb, [xb], [D_XT[bi]])

        def modulate(xb, n, l, which_A, which_sh, r):
            for c in range(NCH):
                ACT(hT[:, c, 0:n], xb[:, c, 0:n], AF.Identity, [xb, DER[l], MOD[l]], [hT],
                    scale=DER[l][:, r, which_A, c:c + 1], bias=mcol(l, which_sh, c, r))

        def ln_stats(v, n):
            def tree(dst, src_is_v):
                h = NCH
                first = True
                while h > 1:
                    h2 = h // 2
                    a = (v if (first and src_is_v) else sqb)
                    if h2 == 1:
                        TT(dst, a[:, 0, 0:n], a[:, 1, 0:n], ALU.add, [a], [s12])
                    else:
                        TT(sqb[:, 0:h2, 0:n], a[:, 0:h2, 0:n], a[:, h2:h, 0:n], ALU.add, [a, sqb], [sqb])
                    first = False
                    h = h2
            assert NCH & (NCH - 1) == 0
            tree(s12[:, 0, 0:n], True)
            ACT(sqb[:, :, 0:n], v[:, :, 0:n], AF.Square, [v], [sqb])
            tree(s12[:, 1, 0:n], False)
            CP(s12h[:, :, 0:n], s12[:, :, 0:n], [s12], [s12h])
            TT(s12r[:, :, 0:n], s12[:, :, 0:n], s12h[:, :, 0:n], ALU.subtract, [s12, s12h], [s12r])
            CP(s12l[:, :, 0:n], s12r[:, :, 0:n], [s12r], [s12l])
            ps = rot("stat", stat_banks[0])
            for q in range(2):
                MM(ps[:, q * BLK:q * BLK + n], onesb[:], s12h[:, q, 0:n], True, False, [onesb, s12h], [ps])
                MM(ps[:, q * BLK:q * BLK + n], onesb[:], s12l[:, q, 0:n], False, True, [onesb, s12l], [ps])
            ACT(lnm[:, 0:n], ps[:, 0:n], AF.Square, [ps], [lnm])
            TT(lnr[:, 0:n], ps[:, BLK:BLK + n], lnm[:, 0:n], ALU.subtract, [ps, lnm], [lnr])
            ACT(lnr[:, 0:n], lnr[:, 0:n], AF.Sqrt, [lnr, epsc], [lnr], bias=epsc[:, 0:1], scale=1.0)
            RECIP(lnr[:, 0:n], lnr[:, 0:n], [lnr], [lnr])
            return ps

        def ln_norm(v, n, ps):
            for c in range(NCH):
                TT(v[:, c, 0:n], v[:, c, 0:n], ps[:, 0:n], ALU.subtract, [v, ps], [v])
                TT(v[:, c, 0:n], v[:, c, 0:n], lnr[:, 0:n], ALU.mult, [v, lnr], [v])

        def resid_prep(xb, n, l, r, with_bias):
            for c in range(NCH):
                bias = DER[l][:, r, 2, c:c + 1] if with_bias else zcol[:, c:c + 1]
                ACT(xb[:, c, 0:n], xb[:, c, 0:n], AF.Identity, [xb, DER[l], zcol], [xb], scale=ALPHA, bias=bias)

        def resid_add(xb, n, c, y_ps_ap, y_ps, l, r, gate_which):
            STT(xb[:, c, 0:n], y_ps_ap, mcol(l, gate_which, c, r), xb[:, c, 0:n], ALU.mult, ALU.add, [y_ps, MOD[l], xb], [xb])

        def resid_finish(xb, n, l, gname, bname):
            ps = ln_stats(xb, n)
            ln_norm(xb, n, ps)
            for c in range(NCH):
                ACT(xb[:, c, 0:n], xb[:, c, 0:n], AF.Identity, [xb, PV[l]], [xb], scale=pvcol(l, gname, c), bias=pvcol(l, bname, c))

        def load_w(dst, dst_view_fn, src_view, nk):
            for k in range(nk):
                dma("pool", dst_view_fn(k), src_view[:, k, :], dst, [D_in], [dst])

        fb = sorted(set([0, min(2, NFC)] + [min(2, NFC) + (NFC - min(2, NFC)) * i // 3 for i in range(1, 4)]))
        FGR = [(fb[i], fb[i + 1]) for i in range(len(fb) - 1) if fb[i + 1] > fb[i]]
        W1g = [mk.buf("w1g%d" % i) for i in range(len(FGR))]
        W3g = [mk.buf("w3g%d" % i) for i in range(len(FGR))]
        AKg, AVg, AQg = mk.buf("wk"), mk.buf("wv"), mk.buf("wq")

        def fgrp(f):
            for i, (a, b_) in enumerate(FGR):
                if a <= f < b_:
                    return i
            raise AssertionError

        def ffn_pass(l, blocks, final):
            with ExitStack() as pst:
                cur[0] = pst
                WA = sb("Fw1", [128, NCH, DFF], BF16)
                WB = sb("Fw3", [128, NCH, DFF], BF16)
                WC = sb("Fw2", [128, NFC, D], BF16)
                uT = sb("uT", [128, NFC, BLK], BF16)
                xbl.append(sb("xb3", [128, NCH, BLK]))
                if l == 0 and DEPTH > 1:
                    slabx = sb("slabx", [128, NCH, 512], BF16)
                    defer_bg[0] = True
                    for l2 in range(1, DEPTH):
                        ada_fn[0](l2, [slabx], PS[5], (l2 - 1) * 12 * NCH)
                    defer_bg[0] = False
                    nslots = max(1, len(blocks) * NFC - NFC)
                    bg_per = (len(bg_pending) + nslots - 1) // nslots
                else:
                    bg_per = 0
                w1v = w1_d[l].rearrange("(k p) n -> p k n", p=128)
                w3v = w3_d[l].rearrange("(k p) n -> p k n", p=128)
                for gi, (fa, fb_) in enumerate(FGR):
                    dma("pool", WA[:, :, fa * 128:fb_ * 128], w1v[:, :, fa * 128:fb_ * 128], W1g[gi], [D_in], [W1g[gi]])
                    dma("pool", WB[:, :, fa * 128:fb_ * 128], w3v[:, :, fa * 128:fb_ * 128], W3g[gi], [D_in], [W3g[gi]])
                load_w(WC, lambda k: WC[:, k, :], w2_d[l].rearrange("(k p) n -> p k n", p=128), NFC)
                nxt = {}

                def s1(bi, n, r, nb):
                    xb = nxt.pop(bi) if bi in nxt else load_xblock(bi, n)
                    if nb is not None:
                        nxt[nb[0]] = load_xblock(nb[0], nb[1])
                    modulate(xb, n, l, 1, 3, r)
                    resid_prep(xb, n, l, r, False)
                    per = (len(pending) + NFC - 1) // NFC if pending else 0
                    for f in range(NFC):
                        ps = rot("gu", [0, 1, 2])
                        for k in range(NCH):
                            MM(ps[:, 0:n], WA[:, k, f * 128:(f + 1) * 128], hT[:, k, 0:n], k == 0, k == NCH - 1, [W1g[fgrp(f)], hT], [ps])
                        for k in range(NCH):
                            MM(ps[:, BLK:BLK + n], WB[:, k, f * 128:(f + 1) * 128], hT[:, k, 0:n], k == 0, k == NCH - 1, [W3g[fgrp(f)], hT], [ps])
                        s_ = sg[f % 2]
                        ACT(s_[:, 0:n], ps[:, 0:n], AF.Silu, [ps], [s_])
                        TT(uT[:, f, 0:n], ps[:, BLK:BLK + n], s_[:, 0:n], ALU.mult, [ps, s_], [uT])
                        drain(per)
                        drain_bg(bg_per)
                    drain()
                    return xb

                def s2(xb, n, r):
                    for c in range(NCH):
                        ps = rot("y", [3, 4])
                        for f in range(NFC):
                            MM(ps[:, 0:n], WC[:, f, c * 128:(c + 1) * 128], uT[:, f, 0:n], f == 0, f == NFC - 1, [WC, uT], [ps])
                        resid_add(xb, n, c, ps[:, 0:n], ps, l, r, 5)

                def s3(xb, bi, n):
                    resid_finish(xb, n, l, "ln_ffn_g", "ln_ffn_b")
                    if not final:
                        store_xblock(xb, bi, n)
                    else:
                        dma("sp", OUT_v[:, :, bi * BLK:bi * BLK + n], xb[:, :, 0:n], xb, [xb], [D_out])

                for i_, (bi, n, r) in enumerate(blocks):
                    nb = blocks[i_ + 1] if i_ + 1 < len(blocks) else None
                    xb = s1(bi, n, r, nb)
                    s2(xb, n, r)
                    defer[0] = True
                    s3(xb, bi, n)
                    defer[0] = False
                drain()
                drain_bg()
                xbl.pop()
                mk.barrier()
            cur[0] = st

        NSLOT = 8
        HPB = 7
        OB = [5, 6, 7]

        def slot_of(tile):
            return NSLOT + (tile - cfg.NT) if tile >= cfg.NT else tile % NSLOT

        def attn_pass(l, slot, ctx_q):
            with ExitStack() as pst:
                cur[0] = pst
                WA = sb("Awqkv", [128, NCH, 3 * D], BF16)
                WB = sb("Awo", [128, NCH, D], BF16)
                kT = sb("kT", [128, NSLOT + 2, NCH, 128], BF16)
                kTb = [mk.buf("kT%d" % i) for i in range(NSLOT + 2)]
                Vr = sb("Vr", [128, NSLOT + 2, NH, 65], BF16)
                Vb = [mk.buf("Vr%d" % i) for i in range(NSLOT + 2)]
                qp = sb("qp", [128, NH, BLK], BF16)
                Eb = sb("Eb", [128, 5, NH * 128], BF16)
                HB = NH * 128 // 2
                bst = sb("bst", [128, HB])
                sexp = [sb("sexp%d" % i, [128, 512]) for i in range(2)]
                PT = [sb("PT%d" % i, [128, 7, 512], BF16) for i in range(2)]
                rec = sb("rec", [128, NH])
                otb = sb("otb", [128, D], BF16)
                oT = sb("oT", [128, NCH, BLK], BF16)
                wqv = wqkv_d[slot].rearrange("(k p) n -> p k n", p=128)
                for (c0_, gb_) in ((D, AKg), (2 * D, AVg), (0, AQg)):
                    dma("pool", WA[:, :, c0_:c0_ + D], wqv[:, :, c0_:c0_ + D], gb_, [D_in], [gb_])
                load_w(WB, lambda k: WB[:, k, :], wo_d[slot].rearrange("(k p) n -> p k n", p=128), NCH)
                for s_ in range(NSLOT + 2):
                    MEMSET(Vr[:, s_, :, 64:65], 1.0, [Vb[s_]])
                MEMSET(qp[:], 0.0, [qp])

                def project_kv(bi, n, r):
                    dma("sp", sqb[:, :, 0:n], xsrc[0][:, :, bi * BLK:bi * BLK + n], sqb, [D_XT[bi]], [sqb])
                    modulate(sqb, n, l, 0, 0, r)
                    for c in range(NCH):
                        ps = rot("misc", [0, 1])
                        for k in range(NCH):
                            MM(ps[:, 0:n], WA[:, k, D + c * 128:D + (c + 1) * 128], hT[:, k, 0:n], k == 0, k == NCH - 1, [AKg, hT], [ps])
                        for ti in range(n // 128):
                            s_ = slot_of(bi * (BLK // 128) + ti)
                            ACT(kT[:, s_, c, :], ps[:, ti * 128:(ti + 1) * 128], AF.Identity, [ps, PV[l]], [kTb[s_]],
                                bias=pvcol(l, "b_k", c), scale=1.0)
                    wdt = min(512, D)
                    for ti in range(n // 128):
                        s_ = slot_of(bi * (BLK // 128) + ti)
                        for half in range(D // wdt):
                            ps = rot("misc", [0, 1])
                            for k in range(NCH):
                                MM(ps[:, 0:wdt], hT[:, k, ti * 128:(ti + 1) * 128], WA[:, k, 2 * D + half * wdt:2 * D + (half + 1) * wdt],
                                   k == 0, k == NCH - 1, [hT, AVg], [ps])
                            nh_ = wdt // 64
                            CP(Vr[:, s_, half * nh_:(half + 1) * nh_, 0:64], ps[:, 0:wdt].rearrange("p (h e) -> p h e", e=64), [ps], [Vb[s_]])

                def project_q(xb, n, r):
                    modulate(xb, n, l, 0, 0, r)
                    for c in range(NCH):
                        ps = rot("misc", [0, 1])
                        for k in range(NCH):
                            MM(ps[:, 0:n], WA[:, k, c * 128:(c + 1) * 128], hT[:, k, 0:n], k == 0, k == NCH - 1, [AQg, hT], [ps])
                        ACT(qp[0:64, 2 * c, 0:n], ps[0:64, 0:n], AF.Identity, [ps, DER[l]], [qp], scale=0.125, bias=DER[l][0:64, 0, 3, c:c + 1])
                        ACT(qp[64:128, 2 * c + 1, 0:n], ps[64:128, 0:n], AF.Identity, [ps, DER[l]], [qp], scale=0.125, bias=DER[l][64:128, 0, 3, c:c + 1])

                cur_regime = [None]

                def load_regime(plist):
                    key = tuple(plist)
                    if cur_regime[0] == key:
                        return
                    cur_regime[0] = key
                    sq2 = sqb[:, :, :].rearrange("p c t -> p (c t)")
                    for i, p in enumerate(plist):
                        for hb in range(2):
                            if (2 * i + hb) % 2 == 0 or NCH * BLK < HB:
                                stg, stg_ap = bst, bst[:]
                            else:
                                stg, stg_ap = sqb, sq2[:, 0:HB]
                            dma("sp", stg_ap, btab_d[slot][p][:, hb * HB:(hb + 1) * HB], stg, [D_in], [stg])
                            ACT(Eb[:, i, hb * HB:(hb + 1) * HB], stg_ap, AF.Exp, [stg], [Eb])

                def attend(qoff, js):
                    nj = len(js)
                    plist = [p for _, p in js if p is not None]
                    if plist:
                        load_regime(plist)
                    HG = min(4, NH)
                    NHG = NH // HG
                    per = (len(pending) + NHG * nj - 1) // (NHG * nj) if pending else 0

                    def qk(hg):
                        pt = PT[hg % 2]
                        li = 0
                        for ji, (j, p) in enumerate(js):
                            s_ = slot_of(j)
                            ps = rot("s", [3, 4])
                            for hh in range(HG):
                                h = hg * HG + hh
                                MM(ps[:, hh * 128:(hh + 1) * 128], kT[:, s_, h // 2, :], qp[:, h, qoff:qoff + 128], True, True, [kTb[s_], qp], [ps])
                            if p is None:
                                ACT(pt[:, ji, 0:HG * 128], ps[:, 0:HG * 128], AF.Exp, [ps], [pt])
                            else:
                                se = sexp[ji % 2]
                                ACT(se[:, 0:HG * 128], ps[:, 0:HG * 128], AF.Exp, [ps], [se])
                                TT(pt[:, ji, 0:HG * 128], se[:, 0:HG * 128], Eb[:, li, hg * HG * 128:(hg + 1) * HG * 128], ALU.mult, [se, Eb], [pt])
                                li += 1
                            drain(per)

                    def pv(hg):
                        pt = PT[hg % 2]
                        for hh in range(HG):
                            h = hg * HG + hh
                            ob = PS[OB[h // HPB]]
                            o0 = (h % HPB) * 65
                            for ji, (j, p) in enumerate(js):
                                s_ = slot_of(j)
                                MM(ob[:, o0:o0 + 65], pt[:, ji, hh * 128:(hh + 1) * 128], Vr[:, s_, h, :], ji == 0, ji == nj - 1, [pt, Vb[s_]], [ob])

                    qk(0)
                    for hg in range(NHG):
                        if hg + 1 < NHG:
                            qk(hg + 1)
                        pv(hg)
                    for b_ in range((NH + HPB - 1) // HPB):
                        nh_ = min(HPB, NH - b_ * HPB)
                        ob = PS[OB[b_]]
                        v3 = ob[:, 0:nh_ * 65].rearrange("p (h e) -> p h e", e=65)
                        RECIP(rec[:, b_ * HPB:b_ * HPB + nh_], v3[:, :, 64], [ob], [rec])
                        for hh in range(nh_):
                            h = b_ * HPB + hh
                            TSM(otb[:, h * 64:(h + 1) * 64], ob[:, hh * 65:hh * 65 + 64], rec[:, h:h + 1], [ob, rec], [otb])
                    for c in range(NCH):
                        ps = rot("misc", [0, 1])
                        MM(ps[:, 0:128], otb[:, c * 128:(c + 1) * 128], identb[:], True, True, [otb, identb], [ps])
                        ACT(oT[:, c, qoff:qoff + 128], ps[:, 0:128], AF.Identity, [ps, PV[l]], [oT], bias=pvcol(l, "b_v", c), scale=1.0)

                def oproj(xb, n, r):
                    resid_prep(xb, n, l, r, True)
                    for c in range(NCH):
                        ps = rot("y", [3, 4])
                        for k in range(NCH):
                            MM(ps[:, 0:n], WB[:, k, c * 128:(c + 1) * 128], oT[:, k, 0:n], k == 0, k == NCH - 1, [WB, oT], [ps])
                        resid_add(xb, n, c, ps[:, 0:n], ps, l, r, 2)

                def finish_deferred(xb, bi, n):
                    defer[0] = True
                    resid_finish(xb, n, l, "ln_mix_g", "ln_mix_b")
                    store_xblock(xb, bi, n)
                    defer[0] = False

                stat_banks[0] = [2]
                nlb = SEQ // BLK
                tpb = BLK // 128
                ctx_js = [(cfg.NT + i, None) for i in range(cfg.NTC)]
                for cb in range(CTX // BLK):
                    bi = nlb + cb
                    project_kv(bi, BLK, 1)
                for cb in range(CTX // BLK if ctx_q else 0):
                    bi = nlb + cb
                    xb = load_xblock(bi, BLK)
                    project_q(xb, BLK, 1)
                    for ti in range(tpb):
                        attend(ti * 128, ctx_js)
                    drain()
                    oproj(xb, BLK, 1)
                    finish_deferred(xb, bi, BLK)
                project_kv(0, BLK, 0)
                if nlb > 1:
                    project_kv(1, BLK, 0)
                xb_next = load_xblock(0, BLK)
                for b in range(nlb):
                    xb = xb_next
                    project_q(xb, BLK, 0)
                    if b + 2 < nlb:
                        defer[0] = True
                        project_kv(b + 2, BLK, 0)
                        defer[0] = False
                    for ti in range(tpb):
                        js = list(per_tile[b * tpb + ti]) + ctx_js
                        attend(ti * 128, js)
                    drain()
                    if b + 1 < nlb:
                        xb_next = load_xblock(b + 1, BLK)
                    oproj(xb, BLK, 0)
                    finish_deferred(xb, b, BLK)
                drain()
                stat_banks[0] = [6, 7]
                mk.barrier()
            cur[0] = st

        def conv_pass(l, slot, ctx_live):
            with ExitStack() as pst:
                cur[0] = pst
                WA = sb("Cpw1", [128, NCH, 2 * D], BF16)
                WB = sb("Cpw2", [128, NCH, D], BF16)
                DG = sb("DG", [128, NCH, 31, 128], BF16)
                ublk = sb("ublk", [128, NCH, BLK], BF16)
                uh = sb("uh", [128, NCH, BLK + 30], BF16)
                cv2 = [sb("cv%d" % i, [128, NCH, BLK]) for i in range(2)]
                sT = sb("sT", [128, NCH, BLK], BF16)
                load_w(WA, lambda k: WA[:, k, :], wpw1_d[slot].rearrange("(k p) n -> p k n", p=128), NCH)
                load_w(WB, lambda k: WB[:, k, :], wpw2_d[slot].rearrange("(k p) n -> p k n", p=128), NCH)
                defer[0] = True
                for c in range(NCH):
                    for k in range(31):
                        TSM(DG[:, c, k, :], identb[:], WDW[slot][:, k * NCH + c:k * NCH + c + 1], [identb, WDW[slot]], [DG])
                defer[0] = False
                nlb = SEQ // BLK
                streams = [(0, [(b, BLK) for b in range(nlb)], UT_v, D_UT)]
                if ctx_live:
                    streams.append((1, [(nlb + b, BLK) for b in range(CTX // BLK)], UC_v, D_UC))
                for (r, blks, Uv, Db) in streams:
                    base = blks[0][0]
                    nxt = {blks[0][0]: load_xblock(*blks[0])}
                    for i_, (bi, n) in enumerate(blks):
                        xb = nxt.pop(bi)
                        if i_ + 1 < len(blks):
                            nxt[blks[i_ + 1][0]] = load_xblock(*blks[i_ + 1])
                        modulate(xb, n, l, 0, 0, r)
                        for c in range(NCH):
                            ps = rot("gu", [0, 1, 2])
                            for k in range(NCH):
                                MM(ps[:, 0:n], WA[:, k, c * 128:(c + 1) * 128], hT[:, k, 0:n], k == 0, k == NCH - 1, [WA, hT], [ps])
                            for k in range(NCH):
                                MM(ps[:, BLK:BLK + n], WA[:, k, D + c * 128:D + (c + 1) * 128], hT[:, k, 0:n], k == 0, k == NCH - 1, [WA, hT], [ps])
                            s_ = sg[c % 2]
                            ACT(s_[:, 0:n], ps[:, BLK:BLK + n], AF.Sigmoid, [ps, PV[l]], [s_], bias=pvcol(l, "b_pw1", NCH + c), scale=1.0)
                            STT(ublk[:, c, 0:n], ps[:, 0:n], pvcol(l, "b_pw1", c), s_[:, 0:n], ALU.add, ALU.mult, [ps, PV[l], s_], [ublk])
                            drain(3)
                        t0 = (bi - base) * BLK
                        dma("sp", Uv[:, :, 15 + t0:15 + t0 + n], ublk[:, :, 0:n], ublk, [ublk], [Db])
                    drain()

                    def c1(bi, n):
                        cvb = cv2[bi % 2]
                        t0 = (bi - base) * BLK
                        dma("sp", uh[:, :, 0:n + 30], Uv[:, :, t0:t0 + n + 30], uh, [Db, D_pad], [uh])
                        per = (len(pending) + NCH - 1) // NCH if pending else 0
                        for c in range(NCH):
                            ps = rot("gu", [0, 1, 2])
                            for k in range(31):
                                MM(ps[:, 0:n], DG[:, c, k, :], uh[:, c, k:k + n], k == 0, k == 30, [DG, uh], [ps])
                            ACT(cvb[:, c, 0:n], ps[:, 0:n], AF.Identity, [ps, PV[l]], [cvb], bias=pvcol(l, "b_dw", c), scale=1.0)
                            drain(per)
                        drain()

                    def c2(bi, n):
                        cvb = cv2[bi % 2]
                        xb = load_xblock(bi, n)
                        ps = ln_stats(cvb, n)
                        ln_norm(cvb, n, ps)
                        for c in range(NCH):
                            ACT(sT[:, c, 0:n], cvb[:, c, 0:n], AF.Silu, [cvb, PV[l]], [sT], scale=pvcol(l, "cln_g", c), bias=pvcol(l, "cln_b", c))
                        resid_prep(xb, n, l, r, True)
                        return xb

                    def c3(xb, n):
                        for c in range(NCH):
                            ps = rot("y", [3, 4])
                            for k in range(NCH):
                                MM(ps[:, 0:n], WB[:, k, c * 128:(c + 1) * 128], sT[:, k, 0:n], k == 0, k == NCH - 1, [WB, sT], [ps])
                            resid_add(xb, n, c, ps[:, 0:n], ps, l, r, 2)

                    def c4(xb, bi, n):
                        resid_finish(xb, n, l, "ln_mix_g", "ln_mix_b")
                        store_xblock(xb, bi, n)

                    c1(*blks[0])
                    for i_, (bi, n) in enumerate(blks):
                        defer[0] = True
                        xb = c2(bi, n)
                        defer[0] = False
                        if i_ + 1 < len(blks):
                            c1(*blks[i_ + 1])
                        drain()
                        c3(xb, n)
                        defer[0] = True
                        c4(xb, bi, n)
                        defer[0] = False
                    drain()
                mk.barrier()
            cur[0] = st

        nlb = SEQ // BLK
        lat_blocks = [(b, BLK, 0) for b in range(nlb)]
        ctx_blocks = [(nlb + b, BLK, 1) for b in range(CTX // BLK)]
        for l in range(DEPTH):
            slot = l // 2
            ctx_live = l < cfg.last_attn
            if l % 2 == 0:
                attn_pass(l, slot, ctx_live)
            else:
                conv_pass(l, slot, ctx_live)
            xsrc[0] = XT_v
            blocks = (ctx_blocks if ctx_live else []) + lat_blocks
            ffn_pass(l, blocks, final=(l == DEPTH - 1))
        mk.final_wait("sp", [D_out])
        mk.emit()
        print("instr counts", mk.cnt, "sems", len(mk.sems), "max sem", mk.max_sem, "max dma", max(mk.dtot.values()), flush=True)
    return nc


_W_NAMES = ["w_ada", "b_ada", "ln_mix_g", "ln_mix_b", "ln_ffn_g", "ln_ffn_b", "attn_w_qkv", "attn_b_qkv", "attn_w_o", "attn_b_o",
            "conv_w_pw1", "conv_b_pw1", "conv_w_dw", "conv_b_dw", "conv_ln_g", "conv_ln_b", "conv_w_pw2", "conv_b_pw2",
            "ffn_w1", "ffn_w3", "ffn_w2"]


def make_in_maps(cfg, inputs):
    x = np.asarray(inputs["x"], dtype=np.float32)
    c = np.asarray(inputs["c"], dtype=np.float32)
    ctx = np.asarray(inputs["ctx"], dtype=np.float32)
    c_ctx = np.asarray(inputs["c_ctx"], dtype=np.float32)
    rpb = np.asarray(inputs["attn_rpb"], dtype=np.float32)
    shared = {k: np.ascontiguousarray(np.asarray(inputs[k], dtype=np.float32)) for k in _W_NAMES}
    shared["btab"] = np.stack([make_btab(rpb[s], cfg) for s in range(rpb.shape[0])], axis=0)
    shared["ident"] = np.eye(128, dtype=np.float32)
    maps = []
    for b in range(x.shape[0]):
        m = dict(shared)
        m["xT"] = np.ascontiguousarray(np.concatenate([x[b].T, ctx[b].T], axis=1))
        m["cc"] = np.ascontiguousarray(np.stack([c[b], c_ctx], axis=0))
        maps.append(m)
    return maps


def kernel(**inputs):
    cfg = Cfg()
    nc = build_nc(cfg)
    maps = make_in_maps(cfg, inputs)
    res = run_bass_kernel_spmd(nc, maps, core_ids=list(range(len(maps))))
    return np.stack([np.ascontiguousarray(np.asarray(r["outT"], dtype=np.float32).T) for r in res.results], axis=0)
```

```python
import numpy as np
from contextlib import ExitStack
import concourse.bass as bass
import concourse.mybir as mybir
from concourse.bass_utils import run_bass_kernel_spmd

F32 = mybir.dt.float32
BF16 = mybir.dt.bfloat16
ALU = mybir.AluOpType
AF = mybir.ActivationFunctionType
AX = mybir.AxisListType

ENGS = ("pe", "act", "dve", "pool", "sp")


class Buf:
    __slots__ = ("name", "lw", "rd", "dsem", "dcnt")

    def __init__(self, name):
        self.name = name
        self.lw = None
        self.rd = []
        self.dsem = None
        self.dcnt = 0


class MK:
    def __init__(self, nc, stack):
        self.nc = nc
        self.stack = stack
        self.sems = {}
        for e in ENGS:
            self.sems["E_" + e] = stack.enter_context(nc.semaphore("s_" + e))
        self.cnt = {e: 0 for e in ENGS}
        self.prog = {e: [] for e in ENGS}
        self.known = {e: {} for e in ENGS}
        self.dtot = {}
        self.nbuf = 0

    def buf(self, name=None):
        self.nbuf += 1
        return Buf("%s_%d" % (name or "b", self.nbuf))

    def _dsem(self, b, eng):
        if b.dsem is None:
            b.dsem = {}
            b.dcnt = {}
        if eng not in b.dsem:
            key = "D_" + b.name + "_" + eng
            self.sems[key] = self.stack.enter_context(self.nc.semaphore("d%d" % len(self.sems)))
            b.dsem[eng] = key
            b.dcnt[eng] = 0
        return b.dsem[eng]

    def _need(self, eng, reads, writes):
        need = {}

        def add(ev, raw):
            if ev is None:
                return
            k, v, e = ev
            if e == eng and k == "E_" + eng:
                if eng in ("pe", "sp"):
                    return
                if self.cnt[eng] - v >= (3 if raw else 6):
                    return
            if need.get(k, 0) < v:
                need[k] = v

        for b in reads:
            add(b.lw, True)
        for b in writes:
            add(b.lw, True)
            for ev in b.rd:
                add(ev, False)
        waits = []
        kn = self.known[eng]
        for k, v in need.items():
            if kn.get(k, 0) < v:
                kn[k] = v
                waits.append((k, v))
        return waits

    def _record(self, ev, reads, writes):
        for b in reads:
            b.rd.append(ev)
            if len(b.rd) > 24:
                best = {}
                for e2 in b.rd:
                    if best.get(e2[0], (0, 0, 0))[1] < e2[1]:
                        best[e2[0]] = e2
                b.rd = list(best.values())
        for b in writes:
            b.lw = ev
            b.rd = []

    def op(self, eng, fn, reads=(), writes=()):
        waits = self._need(eng, reads, writes)
        self.cnt[eng] += 1
        ev = ("E_" + eng, self.cnt[eng], eng)
        self.prog[eng].append((waits, fn, ("E_" + eng, 1)))
        self._record(ev, reads, writes)
        return ev

    def dma(self, eng, out_ap, in_ap, sbuf_buf, reads=(), writes=()):
        waits = self._need(eng, reads, writes)
        key = self._dsem(sbuf_buf, eng)
        sbuf_buf.dcnt[eng] += 16
        self.dtot[key] = sbuf_buf.dcnt[eng]
        ev = (key, sbuf_buf.dcnt[eng], "dma")

        def fn(h, out_ap=out_ap, in_ap=in_ap):
            return h.dma_start(out=out_ap, in_=in_ap)

        self.prog[eng].append((waits, fn, (key, 16)))
        self._record(ev, reads, writes)
        return ev

    def barrier(self):
        for e in ENGS:
            waits = []
            kn = self.known[e]
            for x in ENGS:
                if x == e or self.cnt[x] == 0:
                    continue
                k = "E_" + x
                if kn.get(k, 0) < self.cnt[x]:
                    kn[k] = self.cnt[x]
                    waits.append((k, self.cnt[x]))
            for k, v in self.dtot.items():
                if kn.get(k, 0) < v:
                    kn[k] = v
                    waits.append((k, v))
            if waits:
                self.prog[e].append((waits, None, None))

    def final_wait(self, eng, bufs):
        waits = self._need(eng, bufs, bufs)
        if waits:
            self.prog[eng].append((waits, None, None))

    def emit(self):
        nc = self.nc
        sems = self.sems
        prog = self.prog

        waited = {"E_" + e: set() for e in ENGS}
        for name in ENGS:
            for waits, fn, inc in prog[name]:
                for k, v in waits:
                    if k in waited:
                        waited[k].add(v)
        rank = {k: {v: i + 1 for i, v in enumerate(sorted(vs))} for k, vs in waited.items()}
        self.max_sem = {k: len(vs) for k, vs in waited.items()}

        def run(e, name):
            idx = 0
            for waits, fn, inc in prog[name]:
                for k, v in waits:
                    e.wait_ge(sems[k], rank[k][v] if k in rank else v)
                if fn is not None:
                    ins = fn(e)
                    if inc[0] in rank:
                        idx += 1
                        if idx in rank[inc[0]]:
                            ins.then_inc(sems[inc[0]], 1)
                    else:
                        ins.then_inc(sems[inc[0]], inc[1])

        with nc.Block() as block:
            @block.tensor
            def _(e):
                run(e, "pe")

            @block.scalar
            def _(e):
                run(e, "act")

            @block.vector
            def _(e):
                run(e, "dve")

            @block.gpsimd
            def _(e):
                run(e, "pool")

            @block.sync
            def _(e):
                run(e, "sp")


class T:
    def __init__(self, t, b):
        self.t = t
        self.b = b

    def __getitem__(self, k):
        return self.t[k]


class Cfg:
    def __init__(self, D=1024, SEQ=4096, CTX=256, DFF=2816, DEPTH=4):
        self.D, self.SEQ, self.CTX, self.DFF, self.DEPTH = D, SEQ, CTX, DFF, DEPTH
        self.NCH = D // 128
        self.NH = D // 64
        self.NFC = DFF // 128
        self.ROWS = SEQ // 64
        self.NT = SEQ // 128
        self.NTC = CTX // 128
        self.BLK = 256
        self.NTOK = SEQ + CTX
        self.ALPHA = (2 * DEPTH) ** 0.25
        self.n_attn = len([i for i in range(DEPTH) if i % 2 == 0])
        self.n_conv = DEPTH - self.n_attn
        self.last_attn = max(i for i in range(DEPTH) if i % 2 == 0)
        self.KH = min(8, self.ROWS)


NEG = -1.0e4


def attn_patterns(cfg):
    ROWS, KH = cfg.ROWS, cfg.KH
    sigs = {}
    per_tile = []
    for t in range(cfg.NT):
        rows = (2 * t, 2 * t + 1)
        r0s = [int(np.clip(r - KH // 2, 0, ROWS - KH)) for r in rows]
        j_lo = min(r0s) // 2
        j_hi = (max(r0s) + KH - 1) // 2
        lst = []
        for j in range(j_lo, j_hi + 1):
            sig = []
            for qi, r in enumerate(rows):
                for kr in (2 * j, 2 * j + 1):
                    ok = r0s[qi] <= kr < r0s[qi] + KH
                    sig.append((kr - r + 7) if ok else None)
            sig = tuple(sig)
            if all(s is None for s in sig):
                continue
            if sig not in sigs:
                sigs[sig] = len(sigs)
            lst.append((j, sigs[sig]))
        per_tile.append(lst)
    pats = [None] * len(sigs)
    for s, i in sigs.items():
        pats[i] = s
    return pats, per_tile


def make_btab(rpb, cfg):
    pats, _ = attn_patterns(cfg)
    H = cfg.NH
    c = np.arange(64)
    qs = np.clip(c - 8, 0, 64 - 16)
    kc = np.arange(64)
    colvalid = (kc[:, None] >= qs[None, :]) & (kc[:, None] < qs[None, :] + 16)
    coff = np.clip(kc[:, None] - c[None, :] + 15, 0, 30)
    out = np.full((len(pats), 128, H, 128), NEG, dtype=np.float32)
    for p, sig in enumerate(pats):
        for qi in range(2):
            for ki in range(2):
                ro = sig[qi * 2 + ki]
                if ro is None:
                    continue
                g = rpb[:, ro, :][:, coff]
                g = np.where(colvalid[None], g, np.float32(NEG)).astype(np.float32)
                out[p, ki * 64:(ki + 1) * 64, :, qi * 64:(qi + 1) * 64] = np.transpose(g, (1, 0, 2))
    return np.ascontiguousarray(out.reshape(len(pats), 128, H * 128))


def build_nc(cfg):
    D, NCH, NH, NFC, BLK = cfg.D, cfg.NCH, cfg.NH, cfg.NFC, cfg.BLK
    SEQ, CTX, NTOK, DEPTH, DFF = cfg.SEQ, cfg.CTX, cfg.NTOK, cfg.DEPTH, cfg.DFF
    ALPHA = float(cfg.ALPHA)
    pats, per_tile = attn_patterns(cfg)
    NPAT = len(pats)
    nc = bass.Bass("TRN2", target_bir_lowering=False)

    def din(name, shape):
        return nc.dram_tensor(name, list(shape), F32, kind="ExternalInput").ap()

    xT_d = din("xT", [D, NTOK])
    cc_d = din("cc", [2, D])
    w_ada_d = din("w_ada", [DEPTH, D, 6 * D])
    b_ada_d = din("b_ada", [DEPTH, 6 * D])
    ln_d = {k: din(k, [DEPTH, D]) for k in ("ln_mix_g", "ln_mix_b", "ln_ffn_g", "ln_ffn_b")}
    wqkv_d = din("attn_w_qkv", [cfg.n_attn, D, 3 * D])
    bqkv_d = din("attn_b_qkv", [cfg.n_attn, 3 * D])
    wo_d = din("attn_w_o", [cfg.n_attn, D, D])
    bo_d = din("attn_b_o", [cfg.n_attn, D])
    btab_d = din("btab", [cfg.n_attn, NPAT, 128, NH * 128])
    wpw1_d = din("conv_w_pw1", [cfg.n_conv, D, 2 * D])
    bpw1_d = din("conv_b_pw1", [cfg.n_conv, 2 * D])
    wdw_d = din("conv_w_dw", [cfg.n_conv, 31, D])
    bdw_d = din("conv_b_dw", [cfg.n_conv, D])
    clng_d = din("conv_ln_g", [cfg.n_conv, D])
    clnb_d = din("conv_ln_b", [cfg.n_conv, D])
    wpw2_d = din("conv_w_pw2", [cfg.n_conv, D, D])
    bpw2_d = din("conv_b_pw2", [cfg.n_conv, D])
    w1_d = din("ffn_w1", [DEPTH, D, DFF])
    w3_d = din("ffn_w3", [DEPTH, D, DFF])
    w2_d = din("ffn_w2", [DEPTH, DFF, D])
    ident_d = din("ident", [128, 128])
    out_d = nc.dram_tensor("outT", [D, SEQ], F32, kind="ExternalOutput").ap()
    XIN_v = xT_d.rearrange("(c p) t -> p c t", p=128)
    OUT_v = out_d.rearrange("(c p) t -> p c t", p=128)
    XT_d = nc.dram_tensor("XT", [D, NTOK], F32).ap()
    UT_d = nc.dram_tensor("UT", [D, SEQ + 30], BF16).ap()
    UC_d = nc.dram_tensor("UC", [D, CTX + 30], BF16).ap()
    XT_v = XT_d.rearrange("(c p) t -> p c t", p=128)
    UT_v = UT_d.rearrange("(c p) t -> p c t", p=128)
    UC_v = UC_d.rearrange("(c p) t -> p c t", p=128)

    with ExitStack() as st:
        mk = MK(nc, st)
        cur = [st]

        sbn = [0]

        def sb(name, shape, dt=F32):
            sbn[0] += 1
            return T(cur[0].enter_context(nc.sbuf_tensor("s%d_%s" % (sbn[0], name), list(shape), dt)), mk.buf(name))

        def B(xs):
            return [r.b if isinstance(r, T) else r for r in xs]

        pending = []
        defer = [False]

        def op(eng, fn, rd, wr):
            if defer[0]:
                pending.append(("op", eng, fn, B(rd), B(wr)))
                return None
            return mk.op(eng, fn, B(rd), B(wr))

        def dma(eng, out_ap, in_ap, sbt, rd, wr):
            sb_ = sbt.b if isinstance(sbt, T) else sbt
            if defer[0]:
                pending.append(("dma", eng, out_ap, in_ap, sb_, B(rd), B(wr)))
                return None
            return mk.dma(eng, out_ap, in_ap, sb_, B(rd), B(wr))

        def drain(k=None):
            n = len(pending) if k is None else min(k, len(pending))
            for _ in range(n):
                it = pending.pop(0)
                if it[0] == "op":
                    mk.op(*it[1:])
                else:
                    mk.dma(*it[1:])

        def ACT(out, in_, func, rd, wr, **kw):
            op("act", lambda e: e.activation(out=out, in_=in_, func=func, **kw), rd, wr)

        def MM(out, lhsT, rhs, start, stop, rd, wr):
            op("pe", lambda e: e.matmul(out, lhsT=lhsT, rhs=rhs, start=start, stop=stop), rd, wr)

        def STT(out, in0, scalar, in1, op0, op1, rd, wr):
            op("dve", lambda e: e.scalar_tensor_tensor(out=out, in0=in0, scalar=scalar, in1=in1, op0=op0, op1=op1), rd, wr)

        def TT(out, in0, in1, o, rd, wr):
            op("dve", lambda e: e.tensor_tensor(out=out, in0=in0, in1=in1, op=o), rd, wr)

        def TSA(out, in0, s1, rd, wr):
            op("dve", lambda e: e.tensor_scalar_add(out=out, in0=in0, scalar1=s1), rd, wr)

        def TSM(out, in0, s1, rd, wr):
            op("dve", lambda e: e.tensor_scalar_mul(out=out, in0=in0, scalar1=s1), rd, wr)

        def CP(out, in_, rd, wr):
            op("dve", lambda e: e.tensor_copy(out=out, in_=in_), rd, wr)

        def RECIP(out, in_, rd, wr):
            op("dve", lambda e: e.reciprocal(out=out, in_=in_), rd, wr)

        def MEMSET(ap, val, wr):
            op("dve", lambda e: e.memset(ap, val), [], wr)

        def REDC(out, in3, rd, wr):
            v = in3.rearrange("p c t -> p t c")
            op("dve", lambda e: e.tensor_reduce(out=out, in_=v, axis=AX.X, op=ALU.add), rd, wr)

        D_in = mk.buf("Din")
        D_XT = [mk.buf("XT%d" % i) for i in range(NTOK // BLK)]
        D_UT = mk.buf("UT")
        D_UC = mk.buf("UC")
        D_pad = mk.buf("Upad")
        D_out = mk.buf("Dout")

        PS = [T(st.enter_context(nc.psum_tensor("ps%d" % i, [128, 512], F32)), mk.buf("ps%d" % i)) for i in range(8)]
        rot_state = {}

        def rot(name, banks):
            i = rot_state.get(name, 0)
            rot_state[name] = i + 1
            return PS[banks[i % len(banks)]]

        ident = sb("ident", [128, 128])
        identb = sb("identb", [128, 128], BF16)
        onesb = sb("onesb", [128, 128], BF16)
        epsc = sb("epsc", [128, 1])
        zcol = sb("zcol", [128, NCH])
        dma("sp", ident[:], ident_d, ident, [D_in], [ident])
        CP(identb[:], ident[:], [ident], [identb])
        MEMSET(onesb[:], 1.0 / D, [onesb])
        MEMSET(epsc[:], 1e-5, [epsc])
        MEMSET(zcol[:], 0.0, [zcol])

        PV = [sb("pv%d" % l, [128, 128]) for l in range(DEPTH)]
        WDW = [sb("wdw%d" % s, [128, 31 * NCH]) for s in range(cfg.n_conv)]
        MOD = [sb("mod%d" % l, [128, 6 * NCH, 2]) for l in range(DEPTH)]
        DER = [sb("der%d" % l, [128, 2, 4, NCH]) for l in range(DEPTH)]
        xb2 = [sb("xb%d" % i, [128, NCH, BLK]) for i in range(2)]
        hT = sb("hT", [128, NCH, BLK], BF16)
        sqb = sb("sqb", [128, NCH, BLK])
        s12 = sb("s12", [128, 2, BLK])
        s12h = sb("s12h", [128, 2, BLK], BF16)
        s12l = sb("s12l", [128, 2, BLK], BF16)
        s12r = sb("s12r", [128, 2, BLK])
        lnm = sb("lnm", [128, BLK])
        lnr = sb("lnr", [128, BLK])
        sg = [sb("sg%d" % i, [128, BLK]) for i in range(2)]
        col = []

        def vec_rows(ap1d, width=128):
            return ap1d.rearrange("(n w) -> n w", w=width)

        with ExitStack() as pst:
            cur[0] = pst
            zpad = sb("zpad", [128, NCH, 16], BF16)
            MEMSET(zpad[:], 0.0, [zpad])
            for (v_, n_) in ((UT_v, SEQ), (UC_v, CTX)):
                dma("sp", v_[:, :, 0:15], zpad[:, :, 0:15], zpad, [zpad], [D_pad])
                dma("sp", v_[:, :, 15 + n_:30 + n_], zpad[:, :, 0:15], zpad, [zpad], [D_pad])
            stage = sb("stage", [128, 128])
            sp3 = [sb("sp3_%d" % i, [128, 128], BF16) for i in range(3)]
            spr = sb("spr", [128, 128])

            def rows_to_cols(row_specs, width, dest, dest_col0=0):
                r = 0
                for ap in row_specs:
                    n = ap.shape[0]
                    dma("sp", stage[r:r + n, 0:width], ap, stage, [D_in], [stage])
                    r += n
                R = r
                src = stage[0:R, 0:width]
                CP(sp3[0][0:R, 0:width], src, [stage], [sp3[0]])
                TT(spr[0:R, 0:width], src, sp3[0][0:R, 0:width], ALU.subtract, [stage, sp3[0]], [spr])
                CP(sp3[1][0:R, 0:width], spr[0:R, 0:width], [spr], [sp3[1]])
                TT(spr[0:R, 0:width], spr[0:R, 0:width], sp3[1][0:R, 0:width], ALU.subtract, [spr, sp3[1]], [spr])
                CP(sp3[2][0:R, 0:width], spr[0:R, 0:width], [spr], [sp3[2]])
                ps = rot("misc", [0, 1, 2])
                for i in range(3):
                    MM(ps[0:width, 0:R], sp3[i][0:R, 0:width], identb[0:R, 0:R], i == 0, i == 2, [sp3[i], identb], [ps])
                CP(dest[0:width, dest_col0:dest_col0 + R], ps[0:width, 0:R], [ps], [dest])

            for l in range(DEPTH):
                slot = l // 2
                specs = [("b_ada", vec_rows(b_ada_d[l]))]
                for k in ("ln_mix_g", "ln_mix_b", "ln_ffn_g", "ln_ffn_b"):
                    specs.append((k, vec_rows(ln_d[k][l])))
                if l % 2 == 0:
                    specs.append(("b_q", vec_rows(bqkv_d[slot][0:D])))
                    specs.append(("b_k", vec_rows(bqkv_d[slot][D:2 * D])))
                    specs.append(("b_v", vec_rows(bqkv_d[slot][2 * D:3 * D])))
                    specs.append(("b_out", vec_rows(bo_d[slot])))
                else:
                    specs.append(("b_pw1", vec_rows(bpw1_d[slot])))
                    specs.append(("b_dw", vec_rows(bdw_d[slot])))
                    specs.append(("cln_g", vec_rows(clng_d[slot])))
                    specs.append(("cln_b", vec_rows(clnb_d[slot])))
                    specs.append(("b_out", vec_rows(bpw2_d[slot])))
                cm = {}
                r = 0
                for nm, ap in specs:
                    cm[nm] = r
                    r += ap.shape[0]
                assert r <= 128, r
                col.append(cm)
                rows_to_cols([ap for _, ap in specs], 128, PV[l])
                if l % 2 == 1:
                    wrows = wdw_d[slot].rearrange("k (c w) -> (k c) w", w=128)
                    tot = 31 * NCH
                    r0 = 0
                    while r0 < tot:
                        n = min(124, tot - r0)
                        rows_to_cols([wrows[r0:r0 + n]], 128, WDW[slot], r0)
                        r0 += n

            Sst = sb("Sst", [128, 2 * NCH])
            Sbf = sb("Sbf", [128, NCH, 2], BF16)
            rows_to_cols([vec_rows(cc_d[0]), vec_rows(cc_d[1])], 128, Sst)
            ACT(Sbf[:].rearrange("p c r -> p r c"), Sst[:, 0:2 * NCH].rearrange("p (r c) -> p r c", r=2), AF.Silu, [Sst], [Sbf])
            SLABW = 512
            slabs = [sb("slab%d" % i, [128, NCH, SLABW], BF16) for i in range(2)]
            nslab = 6 * D // SLABW
            si = 0
            per_slab = (len(pending) + DEPTH * nslab - 1) // (DEPTH * nslab)
            for l in range(DEPTH):
                ps = rot("y", [3, 4])
                wv = w_ada_d[l].rearrange("(k p) n -> p k n", p=128)
                for s in range(nslab):
                    sl = slabs[si % 2]
                    si += 1
                    dma("pool", sl[:], wv[:, :, s * SLABW:(s + 1) * SLABW], sl, [D_in], [sl])
                    for c4 in range(SLABW // 128):
                        cc = s * (SLABW // 128) + c4
                        for k in range(NCH):
                            MM(ps[:, cc * 2:cc * 2 + 2], sl[:, k, c4 * 128:(c4 + 1) * 128], Sbf[:, k, :], k == 0, k == NCH - 1, [sl, Sbf], [ps])
                    drain(per_slab)
                for r in range(2):
                    TT(MOD[l][:, :, r], ps[:, 0:12 * NCH].rearrange("p (c r) -> p c r", r=2)[:, :, r],
                       PV[l][:, col[l]["b_ada"]:col[l]["b_ada"] + 6 * NCH], ALU.add, [ps, PV[l]], [MOD[l]])
            for l in range(DEPTH):
                for r in range(2):
                    TSA(DER[l][:, r, 0, :], MOD[l][:, 1 * NCH:2 * NCH, r], 1.0, [MOD[l]], [DER[l]])
                    TSA(DER[l][:, r, 1, :], MOD[l][:, 4 * NCH:5 * NCH, r], 1.0, [MOD[l]], [DER[l]])
                    TT(DER[l][:, r, 2, :], MOD[l][:, 2 * NCH:3 * NCH, r], PV[l][:, col[l]["b_out"]:col[l]["b_out"] + NCH], ALU.mult, [MOD[l], PV[l]], [DER[l]])
                    if l % 2 == 0:
                        TSM(DER[l][:, r, 3, :], PV[l][:, col[l]["b_q"]:col[l]["b_q"] + NCH], 0.125, [PV[l]], [DER[l]])

            drain()
            mk.barrier()
        cur[0] = st

        def mcol(l, which, c, r):
            return MOD[l][:, which * NCH + c:which * NCH + c + 1, r]

        def pvcol(l, name, c):
            o = col[l][name] + c
            return PV[l][:, o:o + 1]

        xb_i = [0]
        stat_banks = [[6, 7]]

        xbl = [xb2[0], xb2[1]]
        xsrc = [XIN_v]

        def load_xblock(bi, n):
            xb = xbl[xb_i[0] % len(xbl)]
            xb_i[0] += 1
            dma("sp", xb[:, :, 0:n], xsrc[0][:, :, bi * BLK:bi * BLK + n], xb, [D_XT[bi]], [xb])
            return xb

        def store_xblock(xb, bi, n):
            dma("sp", XT_v[:, :, bi * BLK:bi * BLK + n], xb[:, :, 0:n], xb, [xb], [D_XT[bi]])

        def modulate(xb, n, l, which_A, which_sh, r):
            for c in range(NCH):
                ACT(hT[:, c, 0:n], xb[:, c, 0:n], AF.Identity, [xb, DER[l], MOD[l]], [hT],
                    scale=DER[l][:, r, which_A, c:c + 1], bias=mcol(l, which_sh, c, r))

        def ln_stats(v, n):
            def tree(dst, src_is_v):
                h = NCH
                first = True
                while h > 1:
                    h2 = h // 2
                    a = (v if (first and src_is_v) else sqb)
                    if h2 == 1:
                        TT(dst, a[:, 0, 0:n], a[:, 1, 0:n], ALU.add, [a], [s12])
                    else:
                        TT(sqb[:, 0:h2, 0:n], a[:, 0:h2, 0:n], a[:, h2:h, 0:n], ALU.add, [a, sqb], [sqb])
                    first = False
                    h = h2
            assert NCH & (NCH - 1) == 0
            tree(s12[:, 0, 0:n], True)
            ACT(sqb[:, :, 0:n], v[:, :, 0:n], AF.Square, [v], [sqb])
            tree(s12[:, 1, 0:n], False)
            CP(s12h[:, :, 0:n], s12[:, :, 0:n], [s12], [s12h])
            TT(s12r[:, :, 0:n], s12[:, :, 0:n], s12h[:, :, 0:n], ALU.subtract, [s12, s12h], [s12r])
            CP(s12l[:, :, 0:n], s12r[:, :, 0:n], [s12r], [s12l])
            ps = rot("stat", stat_banks[0])
            for q in range(2):
                MM(ps[:, q * BLK:q * BLK + n], onesb[:], s12h[:, q, 0:n], True, False, [onesb, s12h], [ps])
                MM(ps[:, q * BLK:q * BLK + n], onesb[:], s12l[:, q, 0:n], False, True, [onesb, s12l], [ps])
            ACT(lnm[:, 0:n], ps[:, 0:n], AF.Square, [ps], [lnm])
            TT(lnr[:, 0:n], ps[:, BLK:BLK + n], lnm[:, 0:n], ALU.subtract, [ps, lnm], [lnr])
            ACT(lnr[:, 0:n], lnr[:, 0:n], AF.Sqrt, [lnr, epsc], [lnr], bias=epsc[:, 0:1], scale=1.0)
            RECIP(lnr[:, 0:n], lnr[:, 0:n], [lnr], [lnr])
            return ps

        def ln_norm(v, n, ps):
            for c in range(NCH):
                TT(v[:, c, 0:n], v[:, c, 0:n], ps[:, 0:n], ALU.subtract, [v, ps], [v])
                TT(v[:, c, 0:n], v[:, c, 0:n], lnr[:, 0:n], ALU.mult, [v, lnr], [v])

        def resid_prep(xb, n, l, r, with_bias):
            for c in range(NCH):
                bias = DER[l][:, r, 2, c:c + 1] if with_bias else zcol[:, c:c + 1]
                ACT(xb[:, c, 0:n], xb[:, c, 0:n], AF.Identity, [xb, DER[l], zcol], [xb], scale=ALPHA, bias=bias)

        def resid_add(xb, n, c, y_ps_ap, y_ps, l, r, gate_which):
            STT(xb[:, c, 0:n], y_ps_ap, mcol(l, gate_which, c, r), xb[:, c, 0:n], ALU.mult, ALU.add, [y_ps, MOD[l], xb], [xb])

        def resid_finish(xb, n, l, gname, bname):
            ps = ln_stats(xb, n)
            ln_norm(xb, n, ps)
            for c in range(NCH):
                ACT(xb[:, c, 0:n], xb[:, c, 0:n], AF.Identity, [xb, PV[l]], [xb], scale=pvcol(l, gname, c), bias=pvcol(l, bname, c))

        def load_w(dst, dst_view_fn, src_view, nk):
            for k in range(nk):
                dma("pool", dst_view_fn(k), src_view[:, k, :], dst, [D_in], [dst])

        fb = sorted(set([0, min(2, NFC)] + [min(2, NFC) + (NFC - min(2, NFC)) * i // 3 for i in range(1, 4)]))
        FGR = [(fb[i], fb[i + 1]) for i in range(len(fb) - 1) if fb[i + 1] > fb[i]]
        W1g = [mk.buf("w1g%d" % i) for i in range(len(FGR))]
        W3g = [mk.buf("w3g%d" % i) for i in range(len(FGR))]
        AKg, AVg, AQg = mk.buf("wk"), mk.buf("wv"), mk.buf("wq")

        def fgrp(f):
            for i, (a, b_) in enumerate(FGR):
                if a <= f < b_:
                    return i
            raise AssertionError

        def ffn_pass(l, blocks, final):
            with ExitStack() as pst:
                cur[0] = pst
                WA = sb("Fw1", [128, NCH, DFF], BF16)
                WB = sb("Fw3", [128, NCH, DFF], BF16)
                WC = sb("Fw2", [128, NFC, D], BF16)
                uT = sb("uT", [128, NFC, BLK], BF16)
                xbl.append(sb("xb3", [128, NCH, BLK]))
                w1v = w1_d[l].rearrange("(k p) n -> p k n", p=128)
                w3v = w3_d[l].rearrange("(k p) n -> p k n", p=128)
                for gi, (fa, fb_) in enumerate(FGR):
                    dma("pool", WA[:, :, fa * 128:fb_ * 128], w1v[:, :, fa * 128:fb_ * 128], W1g[gi], [D_in], [W1g[gi]])
                    dma("pool", WB[:, :, fa * 128:fb_ * 128], w3v[:, :, fa * 128:fb_ * 128], W3g[gi], [D_in], [W3g[gi]])
                load_w(WC, lambda k: WC[:, k, :], w2_d[l].rearrange("(k p) n -> p k n", p=128), NFC)
                nxt = {}

                def s1(bi, n, r, nb):
                    xb = nxt.pop(bi) if bi in nxt else load_xblock(bi, n)
                    if nb is not None:
                        nxt[nb[0]] = load_xblock(nb[0], nb[1])
                    modulate(xb, n, l, 1, 3, r)
                    resid_prep(xb, n, l, r, False)
                    per = (len(pending) + NFC - 1) // NFC if pending else 0
                    for f in range(NFC):
                        ps = rot("gu", [0, 1, 2])
                        for k in range(NCH):
                            MM(ps[:, 0:n], WA[:, k, f * 128:(f + 1) * 128], hT[:, k, 0:n], k == 0, k == NCH - 1, [W1g[fgrp(f)], hT], [ps])
                        for k in range(NCH):
                            MM(ps[:, BLK:BLK + n], WB[:, k, f * 128:(f + 1) * 128], hT[:, k, 0:n], k == 0, k == NCH - 1, [W3g[fgrp(f)], hT], [ps])
                        s_ = sg[f % 2]
                        ACT(s_[:, 0:n], ps[:, 0:n], AF.Silu, [ps], [s_])
                        TT(uT[:, f, 0:n], ps[:, BLK:BLK + n], s_[:, 0:n], ALU.mult, [ps, s_], [uT])
                        drain(per)
                    drain()
                    return xb

                def s2(xb, n, r):
                    for c in range(NCH):
                        ps = rot("y", [3, 4])
                        for f in range(NFC):
                            MM(ps[:, 0:n], WC[:, f, c * 128:(c + 1) * 128], uT[:, f, 0:n], f == 0, f == NFC - 1, [WC, uT], [ps])
                        resid_add(xb, n, c, ps[:, 0:n], ps, l, r, 5)

                def s3(xb, bi, n):
                    resid_finish(xb, n, l, "ln_ffn_g", "ln_ffn_b")
                    if not final:
                        store_xblock(xb, bi, n)
                    else:
                        dma("sp", OUT_v[:, :, bi * BLK:bi * BLK + n], xb[:, :, 0:n], xb, [xb], [D_out])

                for i_, (bi, n, r) in enumerate(blocks):
                    nb = blocks[i_ + 1] if i_ + 1 < len(blocks) else None
                    xb = s1(bi, n, r, nb)
                    s2(xb, n, r)
                    defer[0] = True
                    s3(xb, bi, n)
                    defer[0] = False
                drain()
                xbl.pop()
                mk.barrier()
            cur[0] = st

        NSLOT = 8
        HPB = 7
        OB = [5, 6, 7]

        def slot_of(tile):
            return NSLOT + (tile - cfg.NT) if tile >= cfg.NT else tile % NSLOT

        def attn_pass(l, slot, ctx_q):
            with ExitStack() as pst:
                cur[0] = pst
                WA = sb("Awqkv", [128, NCH, 3 * D], BF16)
                WB = sb("Awo", [128, NCH, D], BF16)
                kT = sb("kT", [128, NSLOT + 2, NCH, 128], BF16)
                kTb = [mk.buf("kT%d" % i) for i in range(NSLOT + 2)]
                Vr = sb("Vr", [128, NSLOT + 2, NH, 65], BF16)
                Vb = [mk.buf("Vr%d" % i) for i in range(NSLOT + 2)]
                qp = sb("qp", [128, NH, BLK], BF16)
                Eb = sb("Eb", [128, 5, NH * 128], BF16)
                HB = NH * 128 // 2
                bst = sb("bst", [128, HB])
                sexp = [sb("sexp%d" % i, [128, 512], BF16) for i in range(2)]
                PT = [sb("PT%d" % i, [128, 7, 512], BF16) for i in range(2)]
                rec = sb("rec", [128, NH])
                otb = sb("otb", [128, D], BF16)
                oT = sb("oT", [128, NCH, BLK], BF16)
                wqv = wqkv_d[slot].rearrange("(k p) n -> p k n", p=128)
                for (c0_, gb_) in ((D, AKg), (2 * D, AVg), (0, AQg)):
                    dma("pool", WA[:, :, c0_:c0_ + D], wqv[:, :, c0_:c0_ + D], gb_, [D_in], [gb_])
                load_w(WB, lambda k: WB[:, k, :], wo_d[slot].rearrange("(k p) n -> p k n", p=128), NCH)
                for s_ in range(NSLOT + 2):
                    MEMSET(Vr[:, s_, :, 64:65], 1.0, [Vb[s_]])
                MEMSET(qp[:], 0.0, [qp])

                def project_kv(bi, n, r):
                    dma("sp", sqb[:, :, 0:n], xsrc[0][:, :, bi * BLK:bi * BLK + n], sqb, [D_XT[bi]], [sqb])
                    modulate(sqb, n, l, 0, 0, r)
                    for c in range(NCH):
                        ps = rot("misc", [0, 1])
                        for k in range(NCH):
                            MM(ps[:, 0:n], WA[:, k, D + c * 128:D + (c + 1) * 128], hT[:, k, 0:n], k == 0, k == NCH - 1, [AKg, hT], [ps])
                        for ti in range(n // 128):
                            s_ = slot_of(bi * (BLK // 128) + ti)
                            ACT(kT[:, s_, c, :], ps[:, ti * 128:(ti + 1) * 128], AF.Identity, [ps, PV[l]], [kTb[s_]],
                                bias=pvcol(l, "b_k", c), scale=1.0)
                    wdt = min(512, D)
                    for ti in range(n // 128):
                        s_ = slot_of(bi * (BLK // 128) + ti)
                        for half in range(D // wdt):
                            ps = rot("misc", [0, 1])
                            for k in range(NCH):
                                MM(ps[:, 0:wdt], hT[:, k, ti * 128:(ti + 1) * 128], WA[:, k, 2 * D + half * wdt:2 * D + (half + 1) * wdt],
                                   k == 0, k == NCH - 1, [hT, AVg], [ps])
                            nh_ = wdt // 64
                            CP(Vr[:, s_, half * nh_:(half + 1) * nh_, 0:64], ps[:, 0:wdt].rearrange("p (h e) -> p h e", e=64), [ps], [Vb[s_]])

                def project_q(xb, n, r):
                    modulate(xb, n, l, 0, 0, r)
                    for c in range(NCH):
                        ps = rot("misc", [0, 1])
                        for k in range(NCH):
                            MM(ps[:, 0:n], WA[:, k, c * 128:(c + 1) * 128], hT[:, k, 0:n], k == 0, k == NCH - 1, [AQg, hT], [ps])
                        ACT(qp[0:64, 2 * c, 0:n], ps[0:64, 0:n], AF.Identity, [ps, DER[l]], [qp], scale=0.125, bias=DER[l][0:64, 0, 3, c:c + 1])
                        ACT(qp[64:128, 2 * c + 1, 0:n], ps[64:128, 0:n], AF.Identity, [ps, DER[l]], [qp], scale=0.125, bias=DER[l][64:128, 0, 3, c:c + 1])

                cur_regime = [None]

                def load_regime(plist):
                    key = tuple(plist)
                    if cur_regime[0] == key:
                        return
                    cur_regime[0] = key
                    sq2 = sqb[:, :, :].rearrange("p c t -> p (c t)")
                    for i, p in enumerate(plist):
                        for hb in range(2):
                            if (2 * i + hb) % 2 == 0 or NCH * BLK < HB:
                                stg, stg_ap = bst, bst[:]
                            else:
                                stg, stg_ap = sqb, sq2[:, 0:HB]
                            dma("sp", stg_ap, btab_d[slot][p][:, hb * HB:(hb + 1) * HB], stg, [D_in], [stg])
                            ACT(Eb[:, i, hb * HB:(hb + 1) * HB], stg_ap, AF.Exp, [stg], [Eb])

                def attend(qoff, js):
                    nj = len(js)
                    plist = [p for _, p in js if p is not None]
                    if plist:
                        load_regime(plist)
                    HG = min(4, NH)
                    NHG = NH // HG
                    per = (len(pending) + NHG * nj - 1) // (NHG * nj) if pending else 0

                    def qk(hg):
                        pt = PT[hg % 2]
                        li = 0
                        for ji, (j, p) in enumerate(js):
                            s_ = slot_of(j)
                            ps = rot("s", [3, 4])
                            for hh in range(HG):
                                h = hg * HG + hh
                                MM(ps[:, hh * 128:(hh + 1) * 128], kT[:, s_, h // 2, :], qp[:, h, qoff:qoff + 128], True, True, [kTb[s_], qp], [ps])
                            if p is None:
                                ACT(pt[:, ji, 0:HG * 128], ps[:, 0:HG * 128], AF.Exp, [ps], [pt])
                            else:
                                se = sexp[ji % 2]
                                ACT(se[:, 0:HG * 128], ps[:, 0:HG * 128], AF.Exp, [ps], [se])
                                TT(pt[:, ji, 0:HG * 128], se[:, 0:HG * 128], Eb[:, li, hg * HG * 128:(hg + 1) * HG * 128], ALU.mult, [se, Eb], [pt])
                                li += 1
                            drain(per)

                    def pv(hg):
                        pt = PT[hg % 2]
                        for hh in range(HG):
                            h = hg * HG + hh
                            ob = PS[OB[h // HPB]]
                            o0 = (h % HPB) * 65
                            for ji, (j, p) in enumerate(js):
                                s_ = slot_of(j)
                                MM(ob[:, o0:o0 + 65], pt[:, ji, hh * 128:(hh + 1) * 128], Vr[:, s_, h, :], ji == 0, ji == nj - 1, [pt, Vb[s_]], [ob])

                    qk(0)
                    for hg in range(NHG):
                        if hg + 1 < NHG:
                            qk(hg + 1)
                        pv(hg)
                    for b_ in range((NH + HPB - 1) // HPB):
                        nh_ = min(HPB, NH - b_ * HPB)
                        ob = PS[OB[b_]]
                        v3 = ob[:, 0:nh_ * 65].rearrange("p (h e) -> p h e", e=65)
                        RECIP(rec[:, b_ * HPB:b_ * HPB + nh_], v3[:, :, 64], [ob], [rec])
                        for hh in range(nh_):
                            h = b_ * HPB + hh
                            TSM(otb[:, h * 64:(h + 1) * 64], ob[:, hh * 65:hh * 65 + 64], rec[:, h:h + 1], [ob, rec], [otb])
                    for c in range(NCH):
                        ps = rot("misc", [0, 1])
                        MM(ps[:, 0:128], otb[:, c * 128:(c + 1) * 128], identb[:], True, True, [otb, identb], [ps])
                        ACT(oT[:, c, qoff:qoff + 128], ps[:, 0:128], AF.Identity, [ps, PV[l]], [oT], bias=pvcol(l, "b_v", c), scale=1.0)

                def oproj(xb, n, r):
                    resid_prep(xb, n, l, r, True)
                    for c in range(NCH):
                        ps = rot("y", [3, 4])
                        for k in range(NCH):
                            MM(ps[:, 0:n], WB[:, k, c * 128:(c + 1) * 128], oT[:, k, 0:n], k == 0, k == NCH - 1, [WB, oT], [ps])
                        resid_add(xb, n, c, ps[:, 0:n], ps, l, r, 2)

                def finish_deferred(xb, bi, n):
                    defer[0] = True
                    resid_finish(xb, n, l, "ln_mix_g", "ln_mix_b")
                    store_xblock(xb, bi, n)
                    defer[0] = False

                stat_banks[0] = [2]
                nlb = SEQ // BLK
                tpb = BLK // 128
                ctx_js = [(cfg.NT + i, None) for i in range(cfg.NTC)]
                for cb in range(CTX // BLK):
                    bi = nlb + cb
                    project_kv(bi, BLK, 1)
                for cb in range(CTX // BLK if ctx_q else 0):
                    bi = nlb + cb
                    xb = load_xblock(bi, BLK)
                    project_q(xb, BLK, 1)
                    for ti in range(tpb):
                        attend(ti * 128, ctx_js)
                    drain()
                    oproj(xb, BLK, 1)
                    finish_deferred(xb, bi, BLK)
                project_kv(0, BLK, 0)
                if nlb > 1:
                    project_kv(1, BLK, 0)
                xb_next = load_xblock(0, BLK)
                for b in range(nlb):
                    xb = xb_next
                    project_q(xb, BLK, 0)
                    if b + 2 < nlb:
                        defer[0] = True
                        project_kv(b + 2, BLK, 0)
                        defer[0] = False
                    for ti in range(tpb):
                        js = list(per_tile[b * tpb + ti]) + ctx_js
                        attend(ti * 128, js)
                    drain()
                    if b + 1 < nlb:
                        xb_next = load_xblock(b + 1, BLK)
                    oproj(xb, BLK, 0)
                    finish_deferred(xb, b, BLK)
                drain()
                stat_banks[0] = [6, 7]
                mk.barrier()
            cur[0] = st

        def conv_pass(l, slot, ctx_live):
            with ExitStack() as pst:
                cur[0] = pst
                WA = sb("Cpw1", [128, NCH, 2 * D], BF16)
                WB = sb("Cpw2", [128, NCH, D], BF16)
                DG = sb("DG", [128, NCH, 31, 128], BF16)
                ublk = sb("ublk", [128, NCH, BLK], BF16)
                uh = sb("uh", [128, NCH, BLK + 30], BF16)
                cv2 = [sb("cv%d" % i, [128, NCH, BLK]) for i in range(2)]
                sT = sb("sT", [128, NCH, BLK], BF16)
                load_w(WA, lambda k: WA[:, k, :], wpw1_d[slot].rearrange("(k p) n -> p k n", p=128), NCH)
                load_w(WB, lambda k: WB[:, k, :], wpw2_d[slot].rearrange("(k p) n -> p k n", p=128), NCH)
                defer[0] = True
                for c in range(NCH):
                    for k in range(31):
                        TSM(DG[:, c, k, :], identb[:], WDW[slot][:, k * NCH + c:k * NCH + c + 1], [identb, WDW[slot]], [DG])
                defer[0] = False
                nlb = SEQ // BLK
                streams = [(0, [(b, BLK) for b in range(nlb)], UT_v, D_UT)]
                if ctx_live:
                    streams.append((1, [(nlb + b, BLK) for b in range(CTX // BLK)], UC_v, D_UC))
                for (r, blks, Uv, Db) in streams:
                    base = blks[0][0]
                    nxt = {blks[0][0]: load_xblock(*blks[0])}
                    for i_, (bi, n) in enumerate(blks):
                        xb = nxt.pop(bi)
                        if i_ + 1 < len(blks):
                            nxt[blks[i_ + 1][0]] = load_xblock(*blks[i_ + 1])
                        modulate(xb, n, l, 0, 0, r)
                        for c in range(NCH):
                            ps = rot("gu", [0, 1, 2])
                            for k in range(NCH):
                                MM(ps[:, 0:n], WA[:, k, c * 128:(c + 1) * 128], hT[:, k, 0:n], k == 0, k == NCH - 1, [WA, hT], [ps])
                            for k in range(NCH):
                                MM(ps[:, BLK:BLK + n], WA[:, k, D + c * 128:D + (c + 1) * 128], hT[:, k, 0:n], k == 0, k == NCH - 1, [WA, hT], [ps])
                            s_ = sg[c % 2]
                            ACT(s_[:, 0:n], ps[:, BLK:BLK + n], AF.Sigmoid, [ps, PV[l]], [s_], bias=pvcol(l, "b_pw1", NCH + c), scale=1.0)
                            STT(ublk[:, c, 0:n], ps[:, 0:n], pvcol(l, "b_pw1", c), s_[:, 0:n], ALU.add, ALU.mult, [ps, PV[l], s_], [ublk])
                            drain(3)
                        t0 = (bi - base) * BLK
                        dma("sp", Uv[:, :, 15 + t0:15 + t0 + n], ublk[:, :, 0:n], ublk, [ublk], [Db])
                    drain()

                    def c1(bi, n):
                        cvb = cv2[bi % 2]
                        t0 = (bi - base) * BLK
                        dma("sp", uh[:, :, 0:n + 30], Uv[:, :, t0:t0 + n + 30], uh, [Db, D_pad], [uh])
                        per = (len(pending) + NCH - 1) // NCH if pending else 0
                        for c in range(NCH):
                            ps = rot("gu", [0, 1, 2])
                            for k in range(31):
                                MM(ps[:, 0:n], DG[:, c, k, :], uh[:, c, k:k + n], k == 0, k == 30, [DG, uh], [ps])
                            ACT(cvb[:, c, 0:n], ps[:, 0:n], AF.Identity, [ps, PV[l]], [cvb], bias=pvcol(l, "b_dw", c), scale=1.0)
                            drain(per)
                        drain()

                    def c2(bi, n):
                        cvb = cv2[bi % 2]
                        xb = load_xblock(bi, n)
                        ps = ln_stats(cvb, n)
                        ln_norm(cvb, n, ps)
                        for c in range(NCH):
                            ACT(sT[:, c, 0:n], cvb[:, c, 0:n], AF.Silu, [cvb, PV[l]], [sT], scale=pvcol(l, "cln_g", c), bias=pvcol(l, "cln_b", c))
                        resid_prep(xb, n, l, r, True)
                        return xb

                    def c3(xb, n):
                        for c in range(NCH):
                            ps = rot("y", [3, 4])
                            for k in range(NCH):
                                MM(ps[:, 0:n], WB[:, k, c * 128:(c + 1) * 128], sT[:, k, 0:n], k == 0, k == NCH - 1, [WB, sT], [ps])
                            resid_add(xb, n, c, ps[:, 0:n], ps, l, r, 2)

                    def c4(xb, bi, n):
                        resid_finish(xb, n, l, "ln_mix_g", "ln_mix_b")
                        store_xblock(xb, bi, n)

                    c1(*blks[0])
                    for i_, (bi, n) in enumerate(blks):
                        defer[0] = True
                        xb = c2(bi, n)
                        defer[0] = False
                        if i_ + 1 < len(blks):
                            c1(*blks[i_ + 1])
                        drain()
                        c3(xb, n)
                        defer[0] = True
                        c4(xb, bi, n)
                        defer[0] = False
                    drain()
                mk.barrier()
            cur[0] = st

        nlb = SEQ // BLK
        lat_blocks = [(b, BLK, 0) for b in range(nlb)]
        ctx_blocks = [(nlb + b, BLK, 1) for b in range(CTX // BLK)]
        for l in range(DEPTH):
            slot = l // 2
            ctx_live = l < cfg.last_attn
            if l % 2 == 0:
                attn_pass(l, slot, ctx_live)
            else:
                conv_pass(l, slot, ctx_live)
            xsrc[0] = XT_v
            blocks = (ctx_blocks if ctx_live else []) + lat_blocks
            ffn_pass(l, blocks, final=(l == DEPTH - 1))
        mk.final_wait("sp", [D_out])
        mk.emit()
        print("instr counts", mk.cnt, "sems", len(mk.sems), "max sem", mk.max_sem, "max dma", max(mk.dtot.values()), flush=True)
    return nc


_W_NAMES = ["w_ada", "b_ada", "ln_mix_g", "ln_mix_b", "ln_ffn_g", "ln_ffn_b", "attn_w_qkv", "attn_b_qkv", "attn_w_o", "attn_b_o",
            "conv_w_pw1", "conv_b_pw1", "conv_w_dw", "conv_b_dw", "conv_ln_g", "conv_ln_b", "conv_w_pw2", "conv_b_pw2",
            "ffn_w1", "ffn_w3", "ffn_w2"]


def make_in_maps(cfg, inputs):
    x = np.asarray(inputs["x"], dtype=np.float32)
    c = np.asarray(inputs["c"], dtype=np.float32)
    ctx = np.asarray(inputs["ctx"], dtype=np.float32)
    c_ctx = np.asarray(inputs["c_ctx"], dtype=np.float32)
    rpb = np.asarray(inputs["attn_rpb"], dtype=np.float32)
    shared = {k: np.ascontiguousarray(np.asarray(inputs[k], dtype=np.float32)) for k in _W_NAMES}
    shared["btab"] = np.stack([make_btab(rpb[s], cfg) for s in range(rpb.shape[0])], axis=0)
    shared["ident"] = np.eye(128, dtype=np.float32)
    maps = []
    for b in range(x.shape[0]):
        m = dict(shared)
        m["xT"] = np.ascontiguousarray(np.concatenate([x[b].T, ctx[b].T], axis=1))
        m["cc"] = np.ascontiguousarray(np.stack([c[b], c_ctx], axis=0))
        maps.append(m)
    return maps


def kernel(**inputs):
    cfg = Cfg()
    nc = build_nc(cfg)
    maps = make_in_maps(cfg, inputs)
    res = run_bass_kernel_spmd(nc, maps, core_ids=list(range(len(maps))))
    return np.stack([np.ascontiguousarray(np.asarray(r["outT"], dtype=np.float32).T) for r in res.results], axis=0)
```

```python
import numpy as np
from contextlib import ExitStack
import concourse.bass as bass
import concourse.mybir as mybir
from concourse.bass_utils import run_bass_kernel_spmd

F32 = mybir.dt.float32
BF16 = mybir.dt.bfloat16
ALU = mybir.AluOpType
AF = mybir.ActivationFunctionType
AX = mybir.AxisListType

ENGS = ("pe", "act", "dve", "pool", "sp")


class Buf:
    __slots__ = ("name", "lw", "rd", "dsem", "dcnt")

    def __init__(self, name):
        self.name = name
        self.lw = None
        self.rd = []
        self.dsem = None
        self.dcnt = 0


class MK:
    def __init__(self, nc, stack):
        self.nc = nc
        self.stack = stack
        self.sems = {}
        for e in ENGS:
            self.sems["E_" + e] = stack.enter_context(nc.semaphore("s_" + e))
        self.cnt = {e: 0 for e in ENGS}
        self.prog = {e: [] for e in ENGS}
        self.known = {e: {} for e in ENGS}
        self.dtot = {}
        self.nbuf = 0

    def buf(self, name=None):
        self.nbuf += 1
        return Buf("%s_%d" % (name or "b", self.nbuf))

    def _dsem(self, b, eng):
        if b.dsem is None:
            b.dsem = {}
            b.dcnt = {}
        if eng not in b.dsem:
            key = "D_" + b.name + "_" + eng
            self.sems[key] = self.stack.enter_context(self.nc.semaphore("d%d" % len(self.sems)))
            b.dsem[eng] = key
            b.dcnt[eng] = 0
        return b.dsem[eng]

    def _need(self, eng, reads, writes):
        need = {}

        def add(ev, raw):
            if ev is None:
                return
            k, v, e = ev
            if e == eng and k == "E_" + eng:
                if eng in ("pe", "sp"):
                    return
                if self.cnt[eng] - v >= (3 if raw else 6):
                    return
            if need.get(k, 0) < v:
                need[k] = v

        for b in reads:
            add(b.lw, True)
        for b in writes:
            add(b.lw, True)
            for ev in b.rd:
                add(ev, False)
        waits = []
        kn = self.known[eng]
        for k, v in need.items():
            if kn.get(k, 0) < v:
                kn[k] = v
                waits.append((k, v))
        return waits

    def _record(self, ev, reads, writes):
        for b in reads:
            b.rd.append(ev)
            if len(b.rd) > 24:
                best = {}
                for e2 in b.rd:
                    if best.get(e2[0], (0, 0, 0))[1] < e2[1]:
                        best[e2[0]] = e2
                b.rd = list(best.values())
        for b in writes:
            b.lw = ev
            b.rd = []

    def op(self, eng, fn, reads=(), writes=()):
        waits = self._need(eng, reads, writes)
        self.cnt[eng] += 1
        ev = ("E_" + eng, self.cnt[eng], eng)
        self.prog[eng].append((waits, fn, ("E_" + eng, 1)))
        self._record(ev, reads, writes)
        return ev

    def dma(self, eng, out_ap, in_ap, sbuf_buf, reads=(), writes=()):
        waits = self._need(eng, reads, writes)
        key = self._dsem(sbuf_buf, eng)
        sbuf_buf.dcnt[eng] += 16
        self.dtot[key] = sbuf_buf.dcnt[eng]
        ev = (key, sbuf_buf.dcnt[eng], "dma")

        def fn(h, out_ap=out_ap, in_ap=in_ap):
            return h.dma_start(out=out_ap, in_=in_ap)

        self.prog[eng].append((waits, fn, (key, 16)))
        self._record(ev, reads, writes)
        return ev

    def barrier(self):
        for e in ENGS:
            waits = []
            kn = self.known[e]
            for x in ENGS:
                if x == e or self.cnt[x] == 0:
                    continue
                k = "E_" + x
                if kn.get(k, 0) < self.cnt[x]:
                    kn[k] = self.cnt[x]
                    waits.append((k, self.cnt[x]))
            for k, v in self.dtot.items():
                if kn.get(k, 0) < v:
                    kn[k] = v
                    waits.append((k, v))
            if waits:
                self.prog[e].append((waits, None, None))

    def final_wait(self, eng, bufs):
        waits = self._need(eng, bufs, bufs)
        if waits:
            self.prog[eng].append((waits, None, None))

    def emit(self):
        nc = self.nc
        sems = self.sems
        prog = self.prog

        waited = {"E_" + e: set() for e in ENGS}
        for name in ENGS:
            for waits, fn, inc in prog[name]:
                for k, v in waits:
                    if k in waited:
                        waited[k].add(v)
        rank = {k: {v: i + 1 for i, v in enumerate(sorted(vs))} for k, vs in waited.items()}
        self.max_sem = {k: len(vs) for k, vs in waited.items()}

        def run(e, name):
            idx = 0
            for waits, fn, inc in prog[name]:
                for k, v in waits:
                    e.wait_ge(sems[k], rank[k][v] if k in rank else v)
                if fn is not None:
                    ins = fn(e)
                    if inc[0] in rank:
                        idx += 1
                        if idx in rank[inc[0]]:
                            ins.then_inc(sems[inc[0]], 1)
                    else:
                        ins.then_inc(sems[inc[0]], inc[1])

        with nc.Block() as block:
            @block.tensor
            def _(e):
                run(e, "pe")

            @block.scalar
            def _(e):
                run(e, "act")

            @block.vector
            def _(e):
                run(e, "dve")

            @block.gpsimd
            def _(e):
                run(e, "pool")

            @block.sync
            def _(e):
                run(e, "sp")


class T:
    def __init__(self, t, b):
        self.t = t
        self.b = b

    def __getitem__(self, k):
        return self.t[k]


class Cfg:
    def __init__(self, D=1024, SEQ=4096, CTX=256, DFF=2816, DEPTH=4):
        self.D, self.SEQ, self.CTX, self.DFF, self.DEPTH = D, SEQ, CTX, DFF, DEPTH
        self.NCH = D // 128
        self.NH = D // 64
        self.NFC = DFF // 128
        self.ROWS = SEQ // 64
        self.NT = SEQ // 128
        self.NTC = CTX // 128
        self.BLK = 256
        self.NTOK = SEQ + CTX
        self.ALPHA = (2 * DEPTH) ** 0.25
        self.n_attn = len([i for i in range(DEPTH) if i % 2 == 0])
        self.n_conv = DEPTH - self.n_attn
        self.last_attn = max(i for i in range(DEPTH) if i % 2 == 0)
        self.KH = min(8, self.ROWS)


NEG = -1.0e4


def attn_patterns(cfg):
    ROWS, KH = cfg.ROWS, cfg.KH
    sigs = {}
    per_tile = []
    for t in range(cfg.NT):
        rows = (2 * t, 2 * t + 1)
        r0s = [int(np.clip(r - KH // 2, 0, ROWS - KH)) for r in rows]
        j_lo = min(r0s) // 2
        j_hi = (max(r0s) + KH - 1) // 2
        lst = []
        for j in range(j_lo, j_hi + 1):
            sig = []
            for qi, r in enumerate(rows):
                for kr in (2 * j, 2 * j + 1):
                    ok = r0s[qi] <= kr < r0s[qi] + KH
                    sig.append((kr - r + 7) if ok else None)
            sig = tuple(sig)
            if all(s is None for s in sig):
                continue
            if sig not in sigs:
                sigs[sig] = len(sigs)
            lst.append((j, sigs[sig]))
        per_tile.append(lst)
    pats = [None] * len(sigs)
    for s, i in sigs.items():
        pats[i] = s
    return pats, per_tile


def make_btab(rpb, cfg):
    pats, _ = attn_patterns(cfg)
    H = cfg.NH
    c = np.arange(64)
    qs = np.clip(c - 8, 0, 64 - 16)
    kc = np.arange(64)
    colvalid = (kc[:, None] >= qs[None, :]) & (kc[:, None] < qs[None, :] + 16)
    coff = np.clip(kc[:, None] - c[None, :] + 15, 0, 30)
    out = np.full((len(pats), 128, H, 128), NEG, dtype=np.float32)
    for p, sig in enumerate(pats):
        for qi in range(2):
            for ki in range(2):
                ro = sig[qi * 2 + ki]
                if ro is None:
                    continue
                g = rpb[:, ro, :][:, coff]
                g = np.where(colvalid[None], g, np.float32(NEG)).astype(np.float32)
                out[p, ki * 64:(ki + 1) * 64, :, qi * 64:(qi + 1) * 64] = np.transpose(g, (1, 0, 2))
    return np.ascontiguousarray(out.reshape(len(pats), 128, H * 128))


def build_nc(cfg):
    D, NCH, NH, NFC, BLK = cfg.D, cfg.NCH, cfg.NH, cfg.NFC, cfg.BLK
    SEQ, CTX, NTOK, DEPTH, DFF = cfg.SEQ, cfg.CTX, cfg.NTOK, cfg.DEPTH, cfg.DFF
    ALPHA = float(cfg.ALPHA)
    pats, per_tile = attn_patterns(cfg)
    NPAT = len(pats)
    nc = bass.Bass("TRN2", target_bir_lowering=False)

    def din(name, shape):
        return nc.dram_tensor(name, list(shape), F32, kind="ExternalInput").ap()

    xT_d = din("xT", [D, NTOK])
    cc_d = din("cc", [2, D])
    w_ada_d = din("w_ada", [DEPTH, D, 6 * D])
    b_ada_d = din("b_ada", [DEPTH, 6 * D])
    ln_d = {k: din(k, [DEPTH, D]) for k in ("ln_mix_g", "ln_mix_b", "ln_ffn_g", "ln_ffn_b")}
    wqkv_d = din("attn_w_qkv", [cfg.n_attn, D, 3 * D])
    bqkv_d = din("attn_b_qkv", [cfg.n_attn, 3 * D])
    wo_d = din("attn_w_o", [cfg.n_attn, D, D])
    bo_d = din("attn_b_o", [cfg.n_attn, D])
    btab_d = din("btab", [cfg.n_attn, NPAT, 128, NH * 128])
    wpw1_d = din("conv_w_pw1", [cfg.n_conv, D, 2 * D])
    bpw1_d = din("conv_b_pw1", [cfg.n_conv, 2 * D])
    wdw_d = din("conv_w_dw", [cfg.n_conv, 31, D])
    bdw_d = din("conv_b_dw", [cfg.n_conv, D])
    clng_d = din("conv_ln_g", [cfg.n_conv, D])
    clnb_d = din("conv_ln_b", [cfg.n_conv, D])
    wpw2_d = din("conv_w_pw2", [cfg.n_conv, D, D])
    bpw2_d = din("conv_b_pw2", [cfg.n_conv, D])
    w1_d = din("ffn_w1", [DEPTH, D, DFF])
    w3_d = din("ffn_w3", [DEPTH, D, DFF])
    w2_d = din("ffn_w2", [DEPTH, DFF, D])
    ident_d = din("ident", [128, 128])
    out_d = nc.dram_tensor("outT", [D, SEQ], F32, kind="ExternalOutput").ap()
    XIN_v = xT_d.rearrange("(c p) t -> p c t", p=128)
    OUT_v = out_d.rearrange("(c p) t -> p c t", p=128)
    XT_d = nc.dram_tensor("XT", [D, NTOK], F32).ap()
    UT_d = nc.dram_tensor("UT", [D, SEQ + 30], BF16).ap()
    UC_d = nc.dram_tensor("UC", [D, CTX + 30], BF16).ap()
    XT_v = XT_d.rearrange("(c p) t -> p c t", p=128)
    UT_v = UT_d.rearrange("(c p) t -> p c t", p=128)
    UC_v = UC_d.rearrange("(c p) t -> p c t", p=128)

    with ExitStack() as st:
        mk = MK(nc, st)
        cur = [st]

        sbn = [0]

        def sb(name, shape, dt=F32):
            sbn[0] += 1
            return T(cur[0].enter_context(nc.sbuf_tensor("s%d_%s" % (sbn[0], name), list(shape), dt)), mk.buf(name))

        def B(xs):
            return [r.b if isinstance(r, T) else r for r in xs]

        pending = []
        defer = [False]

        def op(eng, fn, rd, wr):
            if defer[0]:
                pending.append(("op", eng, fn, B(rd), B(wr)))
                return None
            return mk.op(eng, fn, B(rd), B(wr))

        def dma(eng, out_ap, in_ap, sbt, rd, wr):
            sb_ = sbt.b if isinstance(sbt, T) else sbt
            if defer[0]:
                pending.append(("dma", eng, out_ap, in_ap, sb_, B(rd), B(wr)))
                return None
            return mk.dma(eng, out_ap, in_ap, sb_, B(rd), B(wr))

        def drain(k=None):
            n = len(pending) if k is None else min(k, len(pending))
            for _ in range(n):
                it = pending.pop(0)
                if it[0] == "op":
                    mk.op(*it[1:])
                else:
                    mk.dma(*it[1:])

        def ACT(out, in_, func, rd, wr, **kw):
            op("act", lambda e: e.activation(out=out, in_=in_, func=func, **kw), rd, wr)

        def MM(out, lhsT, rhs, start, stop, rd, wr):
            op("pe", lambda e: e.matmul(out, lhsT=lhsT, rhs=rhs, start=start, stop=stop), rd, wr)

        def STT(out, in0, scalar, in1, op0, op1, rd, wr):
            op("dve", lambda e: e.scalar_tensor_tensor(out=out, in0=in0, scalar=scalar, in1=in1, op0=op0, op1=op1), rd, wr)

        def TT(out, in0, in1, o, rd, wr):
            op("dve", lambda e: e.tensor_tensor(out=out, in0=in0, in1=in1, op=o), rd, wr)

        def TSA(out, in0, s1, rd, wr):
            op("dve", lambda e: e.tensor_scalar_add(out=out, in0=in0, scalar1=s1), rd, wr)

        def TSM(out, in0, s1, rd, wr):
            op("dve", lambda e: e.tensor_scalar_mul(out=out, in0=in0, scalar1=s1), rd, wr)

        def CP(out, in_, rd, wr):
            op("dve", lambda e: e.tensor_copy(out=out, in_=in_), rd, wr)

        def RECIP(out, in_, rd, wr):
            op("dve", lambda e: e.reciprocal(out=out, in_=in_), rd, wr)

        def MEMSET(ap, val, wr):
            op("dve", lambda e: e.memset(ap, val), [], wr)

        def REDC(out, in3, rd, wr):
            v = in3.rearrange("p c t -> p t c")
            op("dve", lambda e: e.tensor_reduce(out=out, in_=v, axis=AX.X, op=ALU.add), rd, wr)

        D_in = mk.buf("Din")
        D_XT = [mk.buf("XT%d" % i) for i in range(NTOK // BLK)]
        D_UT = mk.buf("UT")
        D_UC = mk.buf("UC")
        D_pad = mk.buf("Upad")
        D_out = mk.buf("Dout")

        PS = [T(st.enter_context(nc.psum_tensor("ps%d" % i, [128, 512], F32)), mk.buf("ps%d" % i)) for i in range(8)]
        rot_state = {}

        def rot(name, banks):
            i = rot_state.get(name, 0)
            rot_state[name] = i + 1
            return PS[banks[i % len(banks)]]

        ident = sb("ident", [128, 128])
        identb = sb("identb", [128, 128], BF16)
        onesb = sb("onesb", [128, 128], BF16)
        epsc = sb("epsc", [128, 1])
        zcol = sb("zcol", [128, NCH])
        dma("sp", ident[:], ident_d, ident, [D_in], [ident])
        CP(identb[:], ident[:], [ident], [identb])
        MEMSET(onesb[:], 1.0 / D, [onesb])
        MEMSET(epsc[:], 1e-5, [epsc])
        MEMSET(zcol[:], 0.0, [zcol])

        PV = [sb("pv%d" % l, [128, 128]) for l in range(DEPTH)]
        WDW = [sb("wdw%d" % s, [128, 31 * NCH]) for s in range(cfg.n_conv)]
        MOD = [sb("mod%d" % l, [128, 6 * NCH, 2]) for l in range(DEPTH)]
        DER = [sb("der%d" % l, [128, 2, 4, NCH]) for l in range(DEPTH)]
        xb2 = [sb("xb%d" % i, [128, NCH, BLK]) for i in range(2)]
        hT = sb("hT", [128, NCH, BLK], BF16)
        sqb = sb("sqb", [128, NCH, BLK])
        s12 = sb("s12", [128, 2, BLK])
        s12h = sb("s12h", [128, 2, BLK], BF16)
        s12l = sb("s12l", [128, 2, BLK], BF16)
        s12r = sb("s12r", [128, 2, BLK])
        lnm = sb("lnm", [128, BLK])
        lnr = sb("lnr", [128, BLK])
        sg = [sb("sg%d" % i, [128, BLK]) for i in range(2)]
        col = []

        def vec_rows(ap1d, width=128):
            return ap1d.rearrange("(n w) -> n w", w=width)

        with ExitStack() as pst:
            cur[0] = pst
            zpad = sb("zpad", [128, NCH, 16], BF16)
            MEMSET(zpad[:], 0.0, [zpad])
            for (v_, n_) in ((UT_v, SEQ), (UC_v, CTX)):
                dma("sp", v_[:, :, 0:15], zpad[:, :, 0:15], zpad, [zpad], [D_pad])
                dma("sp", v_[:, :, 15 + n_:30 + n_], zpad[:, :, 0:15], zpad, [zpad], [D_pad])
            stage = sb("stage", [128, 128])
            sp3 = [sb("sp3_%d" % i, [128, 128], BF16) for i in range(3)]
            spr = sb("spr", [128, 128])

            def rows_to_cols(row_specs, width, dest, dest_col0=0):
                r = 0
                for ap in row_specs:
                    n = ap.shape[0]
                    dma("sp", stage[r:r + n, 0:width], ap, stage, [D_in], [stage])
                    r += n
                R = r
                src = stage[0:R, 0:width]
                CP(sp3[0][0:R, 0:width], src, [stage], [sp3[0]])
                TT(spr[0:R, 0:width], src, sp3[0][0:R, 0:width], ALU.subtract, [stage, sp3[0]], [spr])
                CP(sp3[1][0:R, 0:width], spr[0:R, 0:width], [spr], [sp3[1]])
                TT(spr[0:R, 0:width], spr[0:R, 0:width], sp3[1][0:R, 0:width], ALU.subtract, [spr, sp3[1]], [spr])
                CP(sp3[2][0:R, 0:width], spr[0:R, 0:width], [spr], [sp3[2]])
                ps = rot("misc", [0, 1, 2])
                for i in range(3):
                    MM(ps[0:width, 0:R], sp3[i][0:R, 0:width], identb[0:R, 0:R], i == 0, i == 2, [sp3[i], identb], [ps])
                CP(dest[0:width, dest_col0:dest_col0 + R], ps[0:width, 0:R], [ps], [dest])

            for l in range(DEPTH):
                slot = l // 2
                specs = [("b_ada", vec_rows(b_ada_d[l]))]
                for k in ("ln_mix_g", "ln_mix_b", "ln_ffn_g", "ln_ffn_b"):
                    specs.append((k, vec_rows(ln_d[k][l])))
                if l % 2 == 0:
                    specs.append(("b_q", vec_rows(bqkv_d[slot][0:D])))
                    specs.append(("b_k", vec_rows(bqkv_d[slot][D:2 * D])))
                    specs.append(("b_v", vec_rows(bqkv_d[slot][2 * D:3 * D])))
                    specs.append(("b_out", vec_rows(bo_d[slot])))
                else:
                    specs.append(("b_pw1", vec_rows(bpw1_d[slot])))
                    specs.append(("b_dw", vec_rows(bdw_d[slot])))
                    specs.append(("cln_g", vec_rows(clng_d[slot])))
                    specs.append(("cln_b", vec_rows(clnb_d[slot])))
                    specs.append(("b_out", vec_rows(bpw2_d[slot])))
                cm = {}
                r = 0
                for nm, ap in specs:
                    cm[nm] = r
                    r += ap.shape[0]
                assert r <= 128, r
                col.append(cm)
                rows_to_cols([ap for _, ap in specs], 128, PV[l])
                if l % 2 == 1:
                    wrows = wdw_d[slot].rearrange("k (c w) -> (k c) w", w=128)
                    tot = 31 * NCH
                    r0 = 0
                    while r0 < tot:
                        n = min(124, tot - r0)
                        rows_to_cols([wrows[r0:r0 + n]], 128, WDW[slot], r0)
                        r0 += n

            Sst = sb("Sst", [128, 2 * NCH])
            Sbf = sb("Sbf", [128, NCH, 2], BF16)
            rows_to_cols([vec_rows(cc_d[0]), vec_rows(cc_d[1])], 128, Sst)
            ACT(Sbf[:].rearrange("p c r -> p r c"), Sst[:, 0:2 * NCH].rearrange("p (r c) -> p r c", r=2), AF.Silu, [Sst], [Sbf])
            SLABW = 512
            slabs = [sb("slab%d" % i, [128, NCH, SLABW], BF16) for i in range(2)]
            nslab = 6 * D // SLABW
            si = 0
            per_slab = (len(pending) + DEPTH * nslab - 1) // (DEPTH * nslab)
            for l in range(DEPTH):
                ps = rot("y", [3, 4])
                wv = w_ada_d[l].rearrange("(k p) n -> p k n", p=128)
                for s in range(nslab):
                    sl = slabs[si % 2]
                    si += 1
                    dma("pool", sl[:], wv[:, :, s * SLABW:(s + 1) * SLABW], sl, [D_in], [sl])
                    for c4 in range(SLABW // 128):
                        cc = s * (SLABW // 128) + c4
                        for k in range(NCH):
                            MM(ps[:, cc * 2:cc * 2 + 2], sl[:, k, c4 * 128:(c4 + 1) * 128], Sbf[:, k, :], k == 0, k == NCH - 1, [sl, Sbf], [ps])
                    drain(per_slab)
                for r in range(2):
                    TT(MOD[l][:, :, r], ps[:, 0:12 * NCH].rearrange("p (c r) -> p c r", r=2)[:, :, r],
                       PV[l][:, col[l]["b_ada"]:col[l]["b_ada"] + 6 * NCH], ALU.add, [ps, PV[l]], [MOD[l]])
            for l in range(DEPTH):
                for r in range(2):
                    TSA(DER[l][:, r, 0, :], MOD[l][:, 1 * NCH:2 * NCH, r], 1.0, [MOD[l]], [DER[l]])
                    TSA(DER[l][:, r, 1, :], MOD[l][:, 4 * NCH:5 * NCH, r], 1.0, [MOD[l]], [DER[l]])
                    TT(DER[l][:, r, 2, :], MOD[l][:, 2 * NCH:3 * NCH, r], PV[l][:, col[l]["b_out"]:col[l]["b_out"] + NCH], ALU.mult, [MOD[l], PV[l]], [DER[l]])
                    if l % 2 == 0:
                        TSM(DER[l][:, r, 3, :], PV[l][:, col[l]["b_q"]:col[l]["b_q"] + NCH], 0.125, [PV[l]], [DER[l]])

            drain()
            mk.barrier()
        cur[0] = st

        def mcol(l, which, c, r):
            return MOD[l][:, which * NCH + c:which * NCH + c + 1, r]

        def pvcol(l, name, c):
            o = col[l][name] + c
            return PV[l][:, o:o + 1]

        xb_i = [0]
        stat_banks = [[6, 7]]

        xbl = [xb2[0], xb2[1]]
        xsrc = [XIN_v]

        def load_xblock(bi, n):
            xb = xbl[xb_i[0] % len(xbl)]
            xb_i[0] += 1
            dma("sp", xb[:, :, 0:n], xsrc[0][:, :, bi * BLK:bi * BLK + n], xb, [D_XT[bi]], [xb])
            return xb

        def store_xblock(xb, bi, n):
            dma("sp", XT_v[:, :, bi * BLK:bi * BLK + n], xb[:, :, 0:n], xb, [xb], [D_XT[bi]])

        def modulate(xb, n, l, which_A, which_sh, r):
            for c in range(NCH):
                ACT(hT[:, c, 0:n], xb[:, c, 0:n], AF.Identity, [xb, DER[l], MOD[l]], [hT],
                    scale=DER[l][:, r, which_A, c:c + 1], bias=mcol(l, which_sh, c, r))

        def ln_stats(v, n):
            def tree(dst, src_is_v):
                h = NCH
                first = True
                while h > 1:
                    h2 = h // 2
                    a = (v if (first and src_is_v) else sqb)
                    if h2 == 1:
                        TT(dst, a[:, 0, 0:n], a[:, 1, 0:n], ALU.add, [a], [s12])
                    else:
                        TT(sqb[:, 0:h2, 0:n], a[:, 0:h2, 0:n], a[:, h2:h, 0:n], ALU.add, [a, sqb], [sqb])
                    first = False
                    h = h2
            assert NCH & (NCH - 1) == 0
            tree(s12[:, 0, 0:n], True)
            ACT(sqb[:, :, 0:n], v[:, :, 0:n], AF.Square, [v], [sqb])
            tree(s12[:, 1, 0:n], False)
            CP(s12h[:, :, 0:n], s12[:, :, 0:n], [s12], [s12h])
            TT(s12r[:, :, 0:n], s12[:, :, 0:n], s12h[:, :, 0:n], ALU.subtract, [s12, s12h], [s12r])
            CP(s12l[:, :, 0:n], s12r[:, :, 0:n], [s12r], [s12l])
            ps = rot("stat", stat_banks[0])
            for q in range(2):
                MM(ps[:, q * BLK:q * BLK + n], onesb[:], s12h[:, q, 0:n], True, False, [onesb, s12h], [ps])
                MM(ps[:, q * BLK:q * BLK + n], onesb[:], s12l[:, q, 0:n], False, True, [onesb, s12l], [ps])
            ACT(lnm[:, 0:n], ps[:, 0:n], AF.Square, [ps], [lnm])
            TT(lnr[:, 0:n], ps[:, BLK:BLK + n], lnm[:, 0:n], ALU.subtract, [ps, lnm], [lnr])
            ACT(lnr[:, 0:n], lnr[:, 0:n], AF.Ln, [lnr, epsc], [lnr], bias=epsc[:, 0:1], scale=1.0)
            ACT(lnr[:, 0:n], lnr[:, 0:n], AF.Exp, [lnr], [lnr], scale=-0.5)
            return ps

        def ln_norm(v, n, ps):
            for c in range(NCH):
                TT(v[:, c, 0:n], v[:, c, 0:n], ps[:, 0:n], ALU.subtract, [v, ps], [v])
                TT(v[:, c, 0:n], v[:, c, 0:n], lnr[:, 0:n], ALU.mult, [v, lnr], [v])

        def resid_prep(xb, n, l, r, with_bias):
            for c in range(NCH):
                bias = DER[l][:, r, 2, c:c + 1] if with_bias else zcol[:, c:c + 1]
                ACT(xb[:, c, 0:n], xb[:, c, 0:n], AF.Identity, [xb, DER[l], zcol], [xb], scale=ALPHA, bias=bias)

        def resid_add(xb, n, c, y_ps_ap, y_ps, l, r, gate_which):
            STT(xb[:, c, 0:n], y_ps_ap, mcol(l, gate_which, c, r), xb[:, c, 0:n], ALU.mult, ALU.add, [y_ps, MOD[l], xb], [xb])

        def resid_finish(xb, n, l, gname, bname):
            ps = ln_stats(xb, n)
            ln_norm(xb, n, ps)
            for c in range(NCH):
                ACT(xb[:, c, 0:n], xb[:, c, 0:n], AF.Identity, [xb, PV[l]], [xb], scale=pvcol(l, gname, c), bias=pvcol(l, bname, c))

        def load_w(dst, dst_view_fn, src_view, nk):
            for k in range(nk):
                dma("pool", dst_view_fn(k), src_view[:, k, :], dst, [D_in], [dst])

        fb = sorted(set([0, min(2, NFC)] + [min(2, NFC) + (NFC - min(2, NFC)) * i // 3 for i in range(1, 4)]))
        FGR = [(fb[i], fb[i + 1]) for i in range(len(fb) - 1) if fb[i + 1] > fb[i]]
        W1g = [mk.buf("w1g%d" % i) for i in range(len(FGR))]
        W3g = [mk.buf("w3g%d" % i) for i in range(len(FGR))]
        AKg, AVg, AQg = mk.buf("wk"), mk.buf("wv"), mk.buf("wq")

        def fgrp(f):
            for i, (a, b_) in enumerate(FGR):
                if a <= f < b_:
                    return i
            raise AssertionError

        def ffn_pass(l, blocks, final):
            with ExitStack() as pst:
                cur[0] = pst
                WA = sb("Fw1", [128, NCH, DFF], BF16)
                WB = sb("Fw3", [128, NCH, DFF], BF16)
                WC = sb("Fw2", [128, NFC, D], BF16)
                uT = sb("uT", [128, NFC, BLK], BF16)
                xbl.append(sb("xb3", [128, NCH, BLK]))
                w1v = w1_d[l].rearrange("(k p) n -> p k n", p=128)
                w3v = w3_d[l].rearrange("(k p) n -> p k n", p=128)
                for gi, (fa, fb_) in enumerate(FGR):
                    dma("pool", WA[:, :, fa * 128:fb_ * 128], w1v[:, :, fa * 128:fb_ * 128], W1g[gi], [D_in], [W1g[gi]])
                    dma("pool", WB[:, :, fa * 128:fb_ * 128], w3v[:, :, fa * 128:fb_ * 128], W3g[gi], [D_in], [W3g[gi]])
                load_w(WC, lambda k: WC[:, k, :], w2_d[l].rearrange("(k p) n -> p k n", p=128), NFC)
                nxt = {}

                def s1(bi, n, r, nb):
                    xb = nxt.pop(bi) if bi in nxt else load_xblock(bi, n)
                    if nb is not None:
                        nxt[nb[0]] = load_xblock(nb[0], nb[1])
                    modulate(xb, n, l, 1, 3, r)
                    resid_prep(xb, n, l, r, False)
                    per = (len(pending) + NFC - 1) // NFC if pending else 0
                    for f in range(NFC):
                        ps = rot("gu", [0, 1, 2])
                        for k in range(NCH):
                            MM(ps[:, 0:n], WA[:, k, f * 128:(f + 1) * 128], hT[:, k, 0:n], k == 0, k == NCH - 1, [W1g[fgrp(f)], hT], [ps])
                        for k in range(NCH):
                            MM(ps[:, BLK:BLK + n], WB[:, k, f * 128:(f + 1) * 128], hT[:, k, 0:n], k == 0, k == NCH - 1, [W3g[fgrp(f)], hT], [ps])
                        s_ = sg[f % 2]
                        ACT(s_[:, 0:n], ps[:, 0:n], AF.Silu, [ps], [s_])
                        TT(uT[:, f, 0:n], ps[:, BLK:BLK + n], s_[:, 0:n], ALU.mult, [ps, s_], [uT])
                        drain(per)
                    drain()
                    return xb

                def s2(xb, n, r):
                    for c in range(NCH):
                        ps = rot("y", [3, 4])
                        for f in range(NFC):
                            MM(ps[:, 0:n], WC[:, f, c * 128:(c + 1) * 128], uT[:, f, 0:n], f == 0, f == NFC - 1, [WC, uT], [ps])
                        resid_add(xb, n, c, ps[:, 0:n], ps, l, r, 5)

                def s3(xb, bi, n):
                    resid_finish(xb, n, l, "ln_ffn_g", "ln_ffn_b")
                    if not final:
                        store_xblock(xb, bi, n)
                    else:
                        dma("sp", OUT_v[:, :, bi * BLK:bi * BLK + n], xb[:, :, 0:n], xb, [xb], [D_out])

                for i_, (bi, n, r) in enumerate(blocks):
                    nb = blocks[i_ + 1] if i_ + 1 < len(blocks) else None
                    xb = s1(bi, n, r, nb)
                    s2(xb, n, r)
                    defer[0] = True
                    s3(xb, bi, n)
                    defer[0] = False
                drain()
                xbl.pop()
                mk.barrier()
            cur[0] = st

        NSLOT = 8
        HPB = 7
        OB = [5, 6, 7]

        def slot_of(tile):
            return NSLOT + (tile - cfg.NT) if tile >= cfg.NT else tile % NSLOT

        def attn_pass(l, slot, ctx_q):
            with ExitStack() as pst:
                cur[0] = pst
                WA = sb("Awqkv", [128, NCH, 3 * D], BF16)
                WB = sb("Awo", [128, NCH, D], BF16)
                kT = sb("kT", [128, NSLOT + 2, NCH, 128], BF16)
                kTb = [mk.buf("kT%d" % i) for i in range(NSLOT + 2)]
                Vr = sb("Vr", [128, NSLOT + 2, NH, 65], BF16)
                Vb = [mk.buf("Vr%d" % i) for i in range(NSLOT + 2)]
                qp = sb("qp", [128, NH, BLK], BF16)
                Eb = sb("Eb", [128, 5, NH * 128], BF16)
                HB = NH * 128 // 2
                bst = sb("bst", [128, HB])
                sexp = [sb("sexp%d" % i, [128, 512], BF16) for i in range(2)]
                PT = [sb("PT%d" % i, [128, 7, 512], BF16) for i in range(2)]
                rec = sb("rec", [128, NH])
                otb = sb("otb", [128, D], BF16)
                oT = sb("oT", [128, NCH, BLK], BF16)
                wqv = wqkv_d[slot].rearrange("(k p) n -> p k n", p=128)
                for (c0_, gb_) in ((D, AKg), (2 * D, AVg), (0, AQg)):
                    dma("pool", WA[:, :, c0_:c0_ + D], wqv[:, :, c0_:c0_ + D], gb_, [D_in], [gb_])
                load_w(WB, lambda k: WB[:, k, :], wo_d[slot].rearrange("(k p) n -> p k n", p=128), NCH)
                for s_ in range(NSLOT + 2):
                    MEMSET(Vr[:, s_, :, 64:65], 1.0, [Vb[s_]])
                MEMSET(qp[:], 0.0, [qp])

                def project_kv(bi, n, r):
                    dma("sp", sqb[:, :, 0:n], xsrc[0][:, :, bi * BLK:bi * BLK + n], sqb, [D_XT[bi]], [sqb])
                    modulate(sqb, n, l, 0, 0, r)
                    for c in range(NCH):
                        ps = rot("misc", [0, 1])
                        for k in range(NCH):
                            MM(ps[:, 0:n], WA[:, k, D + c * 128:D + (c + 1) * 128], hT[:, k, 0:n], k == 0, k == NCH - 1, [AKg, hT], [ps])
                        for ti in range(n // 128):
                            s_ = slot_of(bi * (BLK // 128) + ti)
                            ACT(kT[:, s_, c, :], ps[:, ti * 128:(ti + 1) * 128], AF.Identity, [ps, PV[l]], [kTb[s_]],
                                bias=pvcol(l, "b_k", c), scale=1.0)
                    wdt = min(512, D)
                    for ti in range(n // 128):
                        s_ = slot_of(bi * (BLK // 128) + ti)
                        for half in range(D // wdt):
                            ps = rot("misc", [0, 1])
                            for k in range(NCH):
                                MM(ps[:, 0:wdt], hT[:, k, ti * 128:(ti + 1) * 128], WA[:, k, 2 * D + half * wdt:2 * D + (half + 1) * wdt],
                                   k == 0, k == NCH - 1, [hT, AVg], [ps])
                            nh_ = wdt // 64
                            CP(Vr[:, s_, half * nh_:(half + 1) * nh_, 0:64], ps[:, 0:wdt].rearrange("p (h e) -> p h e", e=64), [ps], [Vb[s_]])

                def project_q(xb, n, r):
                    modulate(xb, n, l, 0, 0, r)
                    for c in range(NCH):
                        ps = rot("misc", [0, 1])
                        for k in range(NCH):
                            MM(ps[:, 0:n], WA[:, k, c * 128:(c + 1) * 128], hT[:, k, 0:n], k == 0, k == NCH - 1, [AQg, hT], [ps])
                        ACT(qp[0:64, 2 * c, 0:n], ps[0:64, 0:n], AF.Identity, [ps, DER[l]], [qp], scale=0.125, bias=DER[l][0:64, 0, 3, c:c + 1])
                        ACT(qp[64:128, 2 * c + 1, 0:n], ps[64:128, 0:n], AF.Identity, [ps, DER[l]], [qp], scale=0.125, bias=DER[l][64:128, 0, 3, c:c + 1])

                cur_regime = [None]

                def load_regime(plist):
                    key = tuple(plist)
                    if cur_regime[0] == key:
                        return
                    cur_regime[0] = key
                    sq2 = sqb[:, :, :].rearrange("p c t -> p (c t)")
                    for i, p in enumerate(plist):
                        for hb in range(2):
                            if (2 * i + hb) % 2 == 0 or NCH * BLK < HB:
                                stg, stg_ap = bst, bst[:]
                            else:
                                stg, stg_ap = sqb, sq2[:, 0:HB]
                            dma("sp", stg_ap, btab_d[slot][p][:, hb * HB:(hb + 1) * HB], stg, [D_in], [stg])
                            ACT(Eb[:, i, hb * HB:(hb + 1) * HB], stg_ap, AF.Exp, [stg], [Eb])

                def attend(qoff, js):
                    nj = len(js)
                    plist = [p for _, p in js if p is not None]
                    if plist:
                        load_regime(plist)
                    HG = min(4, NH)
                    NHG = NH // HG
                    per = (len(pending) + NHG * nj - 1) // (NHG * nj) if pending else 0

                    def qk(hg):
                        pt = PT[hg % 2]
                        li = 0
                        for ji, (j, p) in enumerate(js):
                            s_ = slot_of(j)
                            ps = rot("s", [3, 4])
                            for hh in range(HG):
                                h = hg * HG + hh
                                MM(ps[:, hh * 128:(hh + 1) * 128], kT[:, s_, h // 2, :], qp[:, h, qoff:qoff + 128], True, True, [kTb[s_], qp], [ps])
                            if p is None:
                                ACT(pt[:, ji, 0:HG * 128], ps[:, 0:HG * 128], AF.Exp, [ps], [pt])
                            else:
                                se = sexp[ji % 2]
                                ACT(se[:, 0:HG * 128], ps[:, 0:HG * 128], AF.Exp, [ps], [se])
                                TT(pt[:, ji, 0:HG * 128], se[:, 0:HG * 128], Eb[:, li, hg * HG * 128:(hg + 1) * HG * 128], ALU.mult, [se, Eb], [pt])
                                li += 1
                            drain(per)

                    def pv(hg):
                        pt = PT[hg % 2]
                        for hh in range(HG):
                            h = hg * HG + hh
                            ob = PS[OB[h // HPB]]
                            o0 = (h % HPB) * 65
                            for ji, (j, p) in enumerate(js):
                                s_ = slot_of(j)
                                MM(ob[:, o0:o0 + 65], pt[:, ji, hh * 128:(hh + 1) * 128], Vr[:, s_, h, :], ji == 0, ji == nj - 1, [pt, Vb[s_]], [ob])

                    qk(0)
                    for hg in range(NHG):
                        if hg + 1 < NHG:
                            qk(hg + 1)
                        pv(hg)
                    for b_ in range((NH + HPB - 1) // HPB):
                        nh_ = min(HPB, NH - b_ * HPB)
                        ob = PS[OB[b_]]
                        v3 = ob[:, 0:nh_ * 65].rearrange("p (h e) -> p h e", e=65)
                        RECIP(rec[:, b_ * HPB:b_ * HPB + nh_], v3[:, :, 64], [ob], [rec])
                        for hh in range(nh_):
                            h = b_ * HPB + hh
                            TSM(otb[:, h * 64:(h + 1) * 64], ob[:, hh * 65:hh * 65 + 64], rec[:, h:h + 1], [ob, rec], [otb])
                    for c in range(NCH):
                        ps = rot("misc", [0, 1])
                        MM(ps[:, 0:128], otb[:, c * 128:(c + 1) * 128], identb[:], True, True, [otb, identb], [ps])
                        ACT(oT[:, c, qoff:qoff + 128], ps[:, 0:128], AF.Identity, [ps, PV[l]], [oT], bias=pvcol(l, "b_v", c), scale=1.0)

                def oproj(xb, n, r):
                    resid_prep(xb, n, l, r, True)
                    for c in range(NCH):
                        ps = rot("y", [3, 4])
                        for k in range(NCH):
                            MM(ps[:, 0:n], WB[:, k, c * 128:(c + 1) * 128], oT[:, k, 0:n], k == 0, k == NCH - 1, [WB, oT], [ps])
                        resid_add(xb, n, c, ps[:, 0:n], ps, l, r, 2)

                def finish_deferred(xb, bi, n):
                    defer[0] = True
                    resid_finish(xb, n, l, "ln_mix_g", "ln_mix_b")
                    store_xblock(xb, bi, n)
                    defer[0] = False

                stat_banks[0] = [2]
                nlb = SEQ // BLK
                tpb = BLK // 128
                ctx_js = [(cfg.NT + i, None) for i in range(cfg.NTC)]
                for cb in range(CTX // BLK):
                    bi = nlb + cb
                    project_kv(bi, BLK, 1)
                for cb in range(CTX // BLK if ctx_q else 0):
                    bi = nlb + cb
                    xb = load_xblock(bi, BLK)
                    project_q(xb, BLK, 1)
                    for ti in range(tpb):
                        attend(ti * 128, ctx_js)
                    drain()
                    oproj(xb, BLK, 1)
                    finish_deferred(xb, bi, BLK)
                project_kv(0, BLK, 0)
                if nlb > 1:
                    project_kv(1, BLK, 0)
                xb_next = load_xblock(0, BLK)
                for b in range(nlb):
                    xb = xb_next
                    project_q(xb, BLK, 0)
                    if b + 2 < nlb:
                        defer[0] = True
                        project_kv(b + 2, BLK, 0)
                        defer[0] = False
                    for ti in range(tpb):
                        js = list(per_tile[b * tpb + ti]) + ctx_js
                        attend(ti * 128, js)
                    drain()
                    if b + 1 < nlb:
                        xb_next = load_xblock(b + 1, BLK)
                    oproj(xb, BLK, 0)
                    finish_deferred(xb, b, BLK)
                drain()
                stat_banks[0] = [6, 7]
                mk.barrier()
            cur[0] = st

        def conv_pass(l, slot, ctx_live):
            with ExitStack() as pst:
                cur[0] = pst
                WA = sb("Cpw1", [128, NCH, 2 * D], BF16)
                WB = sb("Cpw2", [128, NCH, D], BF16)
                DG = sb("DG", [128, NCH, 31, 128], BF16)
                ublk = sb("ublk", [128, NCH, BLK], BF16)
                uh = sb("uh", [128, NCH, BLK + 30], BF16)
                cv2 = [sb("cv%d" % i, [128, NCH, BLK]) for i in range(2)]
                sT = sb("sT", [128, NCH, BLK], BF16)
                load_w(WA, lambda k: WA[:, k, :], wpw1_d[slot].rearrange("(k p) n -> p k n", p=128), NCH)
                load_w(WB, lambda k: WB[:, k, :], wpw2_d[slot].rearrange("(k p) n -> p k n", p=128), NCH)
                defer[0] = True
                for c in range(NCH):
                    for k in range(31):
                        TSM(DG[:, c, k, :], identb[:], WDW[slot][:, k * NCH + c:k * NCH + c + 1], [identb, WDW[slot]], [DG])
                defer[0] = False
                nlb = SEQ // BLK
                streams = [(0, [(b, BLK) for b in range(nlb)], UT_v, D_UT)]
                if ctx_live:
                    streams.append((1, [(nlb + b, BLK) for b in range(CTX // BLK)], UC_v, D_UC))
                for (r, blks, Uv, Db) in streams:
                    base = blks[0][0]
                    nxt = {blks[0][0]: load_xblock(*blks[0])}
                    for i_, (bi, n) in enumerate(blks):
                        xb = nxt.pop(bi)
                        if i_ + 1 < len(blks):
                            nxt[blks[i_ + 1][0]] = load_xblock(*blks[i_ + 1])
                        modulate(xb, n, l, 0, 0, r)
                        for c in range(NCH):
                            ps = rot("gu", [0, 1, 2])
                            for k in range(NCH):
                                MM(ps[:, 0:n], WA[:, k, c * 128:(c + 1) * 128], hT[:, k, 0:n], k == 0, k == NCH - 1, [WA, hT], [ps])
                            for k in range(NCH):
                                MM(ps[:, BLK:BLK + n], WA[:, k, D + c * 128:D + (c + 1) * 128], hT[:, k, 0:n], k == 0, k == NCH - 1, [WA, hT], [ps])
                            s_ = sg[c % 2]
                            ACT(s_[:, 0:n], ps[:, BLK:BLK + n], AF.Sigmoid, [ps, PV[l]], [s_], bias=pvcol(l, "b_pw1", NCH + c), scale=1.0)
                            STT(ublk[:, c, 0:n], ps[:, 0:n], pvcol(l, "b_pw1", c), s_[:, 0:n], ALU.add, ALU.mult, [ps, PV[l], s_], [ublk])
                            drain(3)
                        t0 = (bi - base) * BLK
                        dma("sp", Uv[:, :, 15 + t0:15 + t0 + n], ublk[:, :, 0:n], ublk, [ublk], [Db])
                    drain()

                    def c1(bi, n):
                        cvb = cv2[bi % 2]
                        t0 = (bi - base) * BLK
                        dma("sp", uh[:, :, 0:n + 30], Uv[:, :, t0:t0 + n + 30], uh, [Db, D_pad], [uh])
                        per = (len(pending) + NCH - 1) // NCH if pending else 0
                        for c in range(NCH):
                            ps = rot("gu", [0, 1, 2])
                            for k in range(31):
                                MM(ps[:, 0:n], DG[:, c, k, :], uh[:, c, k:k + n], k == 0, k == 30, [DG, uh], [ps])
                            ACT(cvb[:, c, 0:n], ps[:, 0:n], AF.Identity, [ps, PV[l]], [cvb], bias=pvcol(l, "b_dw", c), scale=1.0)
                            drain(per)
                        drain()

                    def c2(bi, n):
                        cvb = cv2[bi % 2]
                        xb = load_xblock(bi, n)
                        ps = ln_stats(cvb, n)
                        ln_norm(cvb, n, ps)
                        for c in range(NCH):
                            ACT(sT[:, c, 0:n], cvb[:, c, 0:n], AF.Silu, [cvb, PV[l]], [sT], scale=pvcol(l, "cln_g", c), bias=pvcol(l, "cln_b", c))
                        resid_prep(xb, n, l, r, True)
                        return xb

                    def c3(xb, n):
                        for c in range(NCH):
                            ps = rot("y", [3, 4])
                            for k in range(NCH):
                                MM(ps[:, 0:n], WB[:, k, c * 128:(c + 1) * 128], sT[:, k, 0:n], k == 0, k == NCH - 1, [WB, sT], [ps])
                            resid_add(xb, n, c, ps[:, 0:n], ps, l, r, 2)

                    def c4(xb, bi, n):
                        resid_finish(xb, n, l, "ln_mix_g", "ln_mix_b")
                        store_xblock(xb, bi, n)

                    c1(*blks[0])
                    for i_, (bi, n) in enumerate(blks):
                        defer[0] = True
                        xb = c2(bi, n)
                        defer[0] = False
                        if i_ + 1 < len(blks):
                            c1(*blks[i_ + 1])
                        drain()
                        c3(xb, n)
                        defer[0] = True
                        c4(xb, bi, n)
                        defer[0] = False
                    drain()
                mk.barrier()
            cur[0] = st

        nlb = SEQ // BLK
        lat_blocks = [(b, BLK, 0) for b in range(nlb)]
        ctx_blocks = [(nlb + b, BLK, 1) for b in range(CTX // BLK)]
        for l in range(DEPTH):
            slot = l // 2
            ctx_live = l < cfg.last_attn
            if l % 2 == 0:
                attn_pass(l, slot, ctx_live)
            else:
                conv_pass(l, slot, ctx_live)
            xsrc[0] = XT_v
            blocks = (ctx_blocks if ctx_live else []) + lat_blocks
            ffn_pass(l, blocks, final=(l == DEPTH - 1))
        mk.final_wait("sp", [D_out])
        mk.emit()
        print("instr counts", mk.cnt, "sems", len(mk.sems), "max sem", mk.max_sem, "max dma", max(mk.dtot.values()), flush=True)
    return nc


_W_NAMES = ["w_ada", "b_ada", "ln_mix_g", "ln_mix_b", "ln_ffn_g", "ln_ffn_b", "attn_w_qkv", "attn_b_qkv", "attn_w_o", "attn_b_o",
            "conv_w_pw1", "conv_b_pw1", "conv_w_dw", "conv_b_dw", "conv_ln_g", "conv_ln_b", "conv_w_pw2", "conv_b_pw2",
            "ffn_w1", "ffn_w3", "ffn_w2"]


def make_in_maps(cfg, inputs):
    x = np.asarray(inputs["x"], dtype=np.float32)
    c = np.asarray(inputs["c"], dtype=np.float32)
    ctx = np.asarray(inputs["ctx"], dtype=np.float32)
    c_ctx = np.asarray(inputs["c_ctx"], dtype=np.float32)
    rpb = np.asarray(inputs["attn_rpb"], dtype=np.float32)
    shared = {k: np.ascontiguousarray(np.asarray(inputs[k], dtype=np.float32)) for k in _W_NAMES}
    shared["btab"] = np.stack([make_btab(rpb[s], cfg) for s in range(rpb.shape[0])], axis=0)
    shared["ident"] = np.eye(128, dtype=np.float32)
    maps = []
    for b in range(x.shape[0]):
        m = dict(shared)
        m["xT"] = np.ascontiguousarray(np.concatenate([x[b].T, ctx[b].T], axis=1))
        m["cc"] = np.ascontiguousarray(np.stack([c[b], c_ctx], axis=0))
        maps.append(m)
    return maps


def kernel(**inputs):
    cfg = Cfg()
    nc = build_nc(cfg)
    maps = make_in_maps(cfg, inputs)
    res = run_bass_kernel_spmd(nc, maps, core_ids=list(range(len(maps))))
    return np.stack([np.ascontiguousarray(np.asarray(r["outT"], dtype=np.float32).T) for r in res.results], axis=0)
```
